# Optimizing a Trainium2 kernel written in Bass

```python
import math
import jax, jax.numpy as jnp
from jax import lax
import numpy as np

D_MODEL = 1024
BATCH = 4
SEQ = 4096
DEPTH = 2

PLE_DIM = 256

A_WIDTH = D_MODEL // 4
A_DK = 64
A_DV = 64
A_HEADS = A_WIDTH // A_DV
A_CHUNK = 16

B_WIDTH = D_MODEL // 4
B_CH = 64
B_GROUPS = B_WIDTH // B_CH
B_CHUNK = 128

C_WIDTH = D_MODEL // 2
C_NOPE = 64
C_ROPE = 32
C_V = 64
C_HEADS = C_WIDTH // C_V
C_Q_RANK = 384
C_KV_RANK = 256
Q_BLOCK = 128
ROPE_THETA = 10000.0

MIX_WIDTH = A_WIDTH + B_WIDTH + C_WIDTH
IN_SIZES = (A_WIDTH, A_WIDTH, A_WIDTH, A_WIDTH, B_WIDTH, B_WIDTH, C_Q_RANK, C_KV_RANK, C_ROPE)
IN_COLS = sum(IN_SIZES)
IN_SPLITS = tuple(int(s) for s in np.cumsum(IN_SIZES)[:-1])

D_FF = int(math.ceil(8 * D_MODEL / 3 / 256)) * 256
LN_EPS = 1e-5
RMS_EPS = 1e-6
DEEPNORM_ALPHA = (2 * DEPTH) ** 0.25
DEEPNORM_BETA = (8 * DEPTH) ** -0.25

kernel_name = "hybrid_hgrn2_sgu_mla_deepnorm"


def layer_norm(x, g, b):
    xf = x.astype(jnp.float32)
    mu = jnp.mean(xf, -1, keepdims=True)
    var = jnp.mean(jnp.square(xf - mu), -1, keepdims=True)
    return ((xf - mu) * lax.rsqrt(var + LN_EPS)).astype(x.dtype) * g + b


def rms_norm(x, g):
    xf = x.astype(jnp.float32)
    y = xf * lax.rsqrt(jnp.mean(xf * xf, -1, keepdims=True) + RMS_EPS)
    return y.astype(x.dtype) * g


def rope(x, cos, sin):
    x1, x2 = jnp.split(x, 2, axis=-1)
    return jnp.concatenate([x1 * cos - x2 * sin, x2 * cos + x1 * sin], axis=-1).astype(x.dtype)


def hgrn2_mixer(q, f_logit, i_in, g, lb, norm_g):
    bsz, s, _ = q.shape
    n = s // A_CHUNK
    f32 = jnp.float32

    def heads(t):
        return t.astype(f32).reshape(bsz, n, A_CHUNK, A_HEADS, -1)

    lbf = lb.astype(f32)
    f = lbf + (1.0 - lbf) * jax.nn.sigmoid(f_logit.astype(f32))
    qh = heads(jax.nn.silu(q.astype(f32)))
    kh = heads(1.0 - f)
    vh = heads(i_in)
    bcum = jnp.cumsum(heads(jnp.log(f)), axis=2)

    causal = jnp.tril(jnp.ones((A_CHUNK, A_CHUNK), dtype=bool))[None, None, :, :, None, None]
    diff = bcum[:, :, :, None] - bcum[:, :, None, :]
    decay = jnp.exp(jnp.where(causal, diff, -jnp.inf))
    scores = jnp.einsum('bnthk,bnshk,bntshk->bnhts', qh, kh, decay)
    o_intra = jnp.einsum('bnhts,bnshv->bnthv', scores, vh)

    b_last = bcum[:, :, -1]
    k_to_end = kh * jnp.exp(b_last[:, :, None] - bcum)
    chunk_kv = jnp.einsum('bnshk,bnshv->bnhkv', k_to_end, vh)
    chunk_decay = jnp.exp(b_last)

    def step(state, xs):
        dec, kv = xs
        return dec[..., None] * state + kv, state

    init = jnp.zeros((bsz, A_HEADS, A_DK, A_DV), f32)
    _, prev_states = lax.scan(step, init, (jnp.moveaxis(chunk_decay, 1, 0), jnp.moveaxis(chunk_kv, 1, 0)))
    prev_states = jnp.moveaxis(prev_states, 0, 1)
    o_inter = jnp.einsum('bnthk,bnhkv->bnthv', qh * jnp.exp(bcum), prev_states)

    o = (o_intra + o_inter).reshape(bsz, s, A_HEADS, A_DV)
    o = o * lax.rsqrt(jnp.mean(o * o, -1, keepdims=True) + RMS_EPS)
    o = o.reshape(bsz, s, A_WIDTH) * norm_g.astype(f32) * jax.nn.silu(g.astype(f32))
    return o.astype(g.dtype)


def sgu_mixer(u, v, ln_g, ln_b, w_s, b_s):
    bsz, s, _ = u.shape
    n = s // B_CHUNK
    u = jax.nn.gelu(u, approximate=False)
    v = layer_norm(jax.nn.gelu(v, approximate=False), ln_g, ln_b)
    vh = v.reshape(bsz, n, B_CHUNK, B_GROUPS, B_CH)
    w = w_s * jnp.tril(jnp.ones((B_CHUNK, B_CHUNK), dtype=w_s.dtype))
    z = jnp.einsum('gts,bnsgc->bntgc', w, vh) + b_s.T[None, None, :, :, None]
    return u * z.reshape(bsz, s, B_WIDTH)


def mla_mixer(c_q, c_kv, k_rope_raw, cos, sin, q_norm_g, w_uq, kv_norm_g, w_ukv):
    bsz, s, _ = c_q.shape
    q = (rms_norm(c_q, q_norm_g) @ w_uq).reshape(bsz, s, C_HEADS, C_NOPE + C_ROPE)
    q_nope = q[..., :C_NOPE]
    q_rope = rope(q[..., C_NOPE:], cos[:, :, None], sin[:, :, None])
    kv = (rms_norm(c_kv, kv_norm_g) @ w_ukv).reshape(bsz, s, C_HEADS, C_NOPE + C_V)
    k_nope, v = kv[..., :C_NOPE], kv[..., C_NOPE:]
    k_rope = rope(k_rope_raw, cos, sin)

    nb = s // Q_BLOCK
    scale = (C_NOPE + C_ROPE) ** -0.5
    key_idx = jnp.arange(s)

    def blocks(t):
        return jnp.moveaxis(t.reshape(bsz, nb, Q_BLOCK, *t.shape[2:]), 1, 0)

    def attend(args):
        qn, qr, blk = args
        sc = jnp.einsum('bqhd,bkhd->bhqk', qn, k_nope) + jnp.einsum('bqhr,bkr->bhqk', qr, k_rope)
        sc = sc.astype(jnp.float32) * scale
        q_idx = blk * Q_BLOCK + jnp.arange(Q_BLOCK)
        sc = jnp.where(key_idx[None, :] <= q_idx[:, None], sc, -jnp.inf)
        pr = jax.nn.softmax(sc, axis=-1).astype(v.dtype)
        return jnp.einsum('bhqk,bkhv->bqhv', pr, v)

    out = lax.map(attend, (blocks(q_nope), blocks(q_rope), jnp.arange(nb)))
    return jnp.moveaxis(out, 0, 1).reshape(bsz, s, C_WIDTH)


def setup_inputs(seed: int = 0) -> dict:
    key = jax.random.key(seed)
    ks = jax.random.split(key, 32)
    f32 = jnp.float32

    def nrm(k, shape, scale):
        return jax.random.normal(k, shape, f32) * scale

    def gain(k, shape):
        return 1.0 + 0.02 * jax.random.normal(k, shape, f32)

    positions = (jnp.arange(SEQ, dtype=jnp.int32)[None, :]
                 + jax.random.randint(ks[2], (BATCH, 1), 0, 64, dtype=jnp.int32))
    return {
        "x": nrm(ks[0], (BATCH, SEQ, D_MODEL), 1.0),
        "p": nrm(ks[1], (DEPTH, BATCH, SEQ, PLE_DIM), 1.0),
        "positions": positions,
        "ln_in_g": gain(ks[3], (D_MODEL,)),
        "ln_in_b": nrm(ks[4], (D_MODEL,), 0.02),
        "w_in": nrm(ks[5], (DEPTH, D_MODEL, IN_COLS), D_MODEL ** -0.5),
        "hgrn_lb_logits": nrm(ks[6], (DEPTH, A_WIDTH), 0.5),
        "hgrn_norm_g": gain(ks[7], (DEPTH, A_WIDTH)),
        "sgu_ln_g": gain(ks[8], (DEPTH, B_WIDTH)),
        "sgu_ln_b": nrm(ks[9], (DEPTH, B_WIDTH), 0.02),
        "sgu_w_s": nrm(ks[10], (DEPTH, B_GROUPS, B_CHUNK, B_CHUNK), B_CHUNK ** -0.5),
        "sgu_b_s": gain(ks[11], (DEPTH, B_GROUPS, B_CHUNK)),
        "mla_q_norm_g": gain(ks[12], (DEPTH, C_Q_RANK)),
        "mla_w_uq": nrm(ks[13], (DEPTH, C_Q_RANK, C_HEADS * (C_NOPE + C_ROPE)), C_Q_RANK ** -0.5),
        "mla_kv_norm_g": gain(ks[14], (DEPTH, C_KV_RANK)),
        "mla_w_ukv": nrm(ks[15], (DEPTH, C_KV_RANK, C_HEADS * (C_NOPE + C_V)), C_KV_RANK ** -0.5),
        "w_out": nrm(ks[16], (DEPTH, MIX_WIDTH, D_MODEL), DEEPNORM_BETA * MIX_WIDTH ** -0.5),
        "ln1_g": gain(ks[17], (DEPTH, D_MODEL)),
        "ln1_b": nrm(ks[18], (DEPTH, D_MODEL), 0.02),
        "w_gate_up": nrm(ks[19], (DEPTH, D_MODEL, 2 * D_FF), D_MODEL ** -0.5),
        "w_down": nrm(ks[20], (DEPTH, D_FF, D_MODEL), DEEPNORM_BETA * D_FF ** -0.5),
        "ple_w_gate": nrm(ks[21], (DEPTH, D_MODEL, D_MODEL), D_MODEL ** -0.5),
        "ple_w_proj": nrm(ks[22], (DEPTH, PLE_DIM, D_MODEL), DEEPNORM_BETA * PLE_DIM ** -0.5),
        "ln2_g": gain(ks[23], (DEPTH, D_MODEL)),
        "ln2_b": nrm(ks[24], (DEPTH, D_MODEL), 0.02),
    }


def reference(x, p, positions, ln_in_g, ln_in_b, w_in, hgrn_lb_logits, hgrn_norm_g,
              sgu_ln_g, sgu_ln_b, sgu_w_s, sgu_b_s, mla_q_norm_g, mla_w_uq,
              mla_kv_norm_g, mla_w_ukv, w_out, ln1_g, ln1_b, w_gate_up, w_down,
              ple_w_gate, ple_w_proj, ln2_g, ln2_b):
    lb_cum = jnp.cumsum(jax.nn.softmax(hgrn_lb_logits.astype(jnp.float32), axis=0), axis=0)
    lower_bounds = lb_cum - lb_cum[0]

    inv_freq = ROPE_THETA ** (-jnp.arange(0, C_ROPE, 2, dtype=jnp.float32) / C_ROPE)
    ang = positions.astype(jnp.float32)[..., None] * inv_freq
    cos, sin = jnp.cos(ang), jnp.sin(ang)

    h = layer_norm(x, ln_in_g, ln_in_b)
    for i in range(DEPTH):
        proj = h @ w_in[i]
        a_q, a_f, a_i, a_g, b_u, b_v, c_q, c_kv, c_kr = jnp.split(proj, IN_SPLITS, axis=-1)
        o_a = hgrn2_mixer(a_q, a_f, a_i, a_g, lower_bounds[i], hgrn_norm_g[i])
        o_b = sgu_mixer(b_u, b_v, sgu_ln_g[i], sgu_ln_b[i], sgu_w_s[i], sgu_b_s[i])
        o_c = mla_mixer(c_q, c_kv, c_kr, cos, sin, mla_q_norm_g[i], mla_w_uq[i],
                        mla_kv_norm_g[i], mla_w_ukv[i])
        mix = jnp.concatenate([o_a, o_b, o_c], axis=-1) @ w_out[i]
        h = layer_norm(DEEPNORM_ALPHA * h + mix, ln1_g[i], ln1_b[i])

        gate, up = jnp.split(h @ w_gate_up[i], 2, axis=-1)
        ffn = (jax.nn.silu(gate) * up) @ w_down[i]
        ple = jax.nn.sigmoid(h @ ple_w_gate[i]) * (p[i] @ ple_w_proj[i])
        h = layer_norm(DEEPNORM_ALPHA * h + ffn + ple, ln2_g[i], ln2_b[i])
    return h
```

```python
import os
import contextlib
import numpy as np
import concourse.bass as bass
import concourse.mybir as mybir
from concourse.bass_utils import run_bass_kernel_spmd

F32 = mybir.dt.float32
BF16 = mybir.dt.bfloat16
I32 = mybir.dt.int32
AF = mybir.ActivationFunctionType
ALU = mybir.AluOpType
AX = mybir.AxisListType

S = 4096
D = 1024
NB = 8
NT = 32
DFF = 2816
NJ = 22
ALPHA = 4.0 ** 0.25
LN_EPS = 1e-5
RMS_EPS = 1e-6
ENGS = ["pe", "act", "dve", "pool", "sp"]
SAME_ENGINE_SYNC = os.environ.get('KSES', '1') == '1'
TWO_PI = 2.0 * np.pi
C1 = 6.28125
C2 = float(TWO_PI - 6.28125)


class Res:
    __slots__ = ("name", "w", "rs")

    def __init__(self, name=""):
        self.name = name
        self.w = None
        self.rs = []


class Op:
    __slots__ = ("eng", "fn", "deps", "is_dma", "signal", "sigval", "dsem", "dval", "dprev")

    def __init__(self, eng, fn, is_dma):
        self.eng = eng
        self.fn = fn
        self.deps = []
        self.is_dma = is_dma
        self.signal = False
        self.sigval = 0
        self.dsem = None
        self.dval = 0
        self.dprev = 0


class Prog:
    def __init__(self, n_dma_sems=8):
        self.ops = {e: [] for e in ENGS}
        self.n_dma_sems = n_dma_sems
        self.dma_count = {e: 0 for e in ENGS}
        self.last_dma = {}
        self.pending_bar = {e: None for e in ENGS}
        self.dummy = None

    def _add(self, op, reads, writes):
        deps = []
        raw = set()
        for r in reads:
            if r.w is not None:
                deps.append(r.w)
                raw.add(id(r.w))
        for w in writes:
            if w.w is not None:
                deps.append(w.w)
            deps.extend(w.rs)
        if self.pending_bar[op.eng] is not None:
            deps.extend(self.pending_bar[op.eng])
            self.pending_bar[op.eng] = None
        seen = set()
        for d in deps:
            if id(d) in seen or d is op:
                continue
            seen.add(id(d))
            if d.eng == op.eng and not d.is_dma and not op.is_dma and (op.eng == "pe" or not SAME_ENGINE_SYNC
                                                                        or id(d) not in raw):
                continue
            op.deps.append(d)
        for r in reads:
            r.rs.append(op)
        for w in writes:
            w.w = op
            w.rs = []
        self.ops[op.eng].append(op)
        return op

    def op(self, eng, fn, reads=(), writes=()):
        return self._add(Op(eng, fn, False), reads, writes)

    def dma(self, eng, fn, reads=(), writes=()):
        op = Op(eng, fn, True)
        op.dsem = (eng, self.dma_count[eng] % self.n_dma_sems)
        self.dma_count[eng] += 1
        self.last_dma[op.dsem] = op
        return self._add(op, reads, writes)

    def barrier(self):
        lasts = [self.ops[e][-1] for e in ENGS if self.ops[e] and not self.ops[e][-1].is_dma]
        lasts = []
        for e in ENGS:
            for o in reversed(self.ops[e]):
                if not o.is_dma:
                    lasts.append(o)
                    break
        lasts.extend(self.last_dma.values())
        for e in ENGS:
            prev = self.pending_bar[e] or []
            self.pending_bar[e] = list(prev) + lasts

    def emit(self, nc):
        for e in ENGS:
            for op in self.ops[e]:
                for d in op.deps:
                    if not d.is_dma:
                        d.signal = True
        for e in ENGS:
            n = 0
            for op in self.ops[e]:
                if op.signal and not op.is_dma:
                    n += 1
                    op.sigval = n
        cnt = {}
        for e in ENGS:
            for op in self.ops[e]:
                if op.is_dma:
                    c = cnt.get(op.dsem, 0)
                    op.dprev = c
                    op.dval = c + 16
                    cnt[op.dsem] = c + 16
        with contextlib.ExitStack() as st:
            esem = {e: st.enter_context(nc.semaphore("s_" + e)) for e in ENGS}
            dsem = {k: st.enter_context(nc.semaphore("d_%s_%d" % k)) for k in sorted(cnt.keys())}
            block = st.enter_context(nc.Block())

            def run_engine(ename, eng):
                waited = {}
                if ename == "act" and self.dummy is not None and ("sp", 0) in dsem:
                    eng.wait_ge(dsem[("sp", 0)], 16)
                    waited[("d", ("sp", 0))] = 16
                for op in self.ops[ename]:
                    if op.is_dma and op.dprev > 0:
                        key = ("d", op.dsem)
                        if waited.get(key, 0) < op.dprev:
                            eng.wait_ge(dsem[op.dsem], op.dprev)
                            waited[key] = op.dprev
                    for d in op.deps:
                        if d.is_dma:
                            key = ("d", d.dsem)
                            if waited.get(key, 0) < d.dval:
                                eng.wait_ge(dsem[d.dsem], d.dval)
                                waited[key] = d.dval
                        else:
                            key = ("e", d.eng)
                            if waited.get(key, 0) < d.sigval:
                                eng.wait_ge(esem[d.eng], d.sigval)
                                waited[key] = d.sigval
                    ins = op.fn(eng)
                    if op.is_dma:
                        ins.then_inc(dsem[op.dsem], 16)
                    elif op.signal:
                        if ename in ("dve", "pool") and self.dummy is not None:
                            ins = eng.memset(self.dummy[ename], 0.0)
                        elif ename == "act" and self.dummy is not None:
                            ins = eng.copy(out=self.dummy["act"], in_=self.dummy["act_src"])
                        ins.then_inc(esem[ename], 1)
                if ename == "sp":
                    for k, v in cnt.items():
                        eng.wait_ge(dsem[k], v)

            @block.tensor
            def _(eng):
                run_engine("pe", eng)

            @block.scalar
            def _(eng):
                run_engine("act", eng)

            @block.vector
            def _(eng):
                run_engine("dve", eng)

            @block.gpsimd
            def _(eng):
                run_engine("pool", eng)

            @block.sync
            def _(eng):
                run_engine("sp", eng)


class Tile:
    __slots__ = ("ap", "r")

    def __init__(self, ap, name=""):
        self.ap = ap
        self.r = Res(name)

    def v3(self, a):
        return self.ap.rearrange("p (a b) -> p a b", a=a)

    def v4(self, a, b):
        return self.ap.rearrange("p (a b c) -> p a b c", a=a, b=b)


class Arena:
    def __init__(self, t, n):
        self.t = t
        self.n = n
        self.off = 0
        self.base = 0

    def freeze(self):
        self.base = self.off

    def reset(self):
        self.off = self.base

    def alloc(self, cols, parts=128, name=""):
        assert self.off + cols <= self.n, ("arena overflow", name, self.off, cols, self.n)
        v = self.t[0:parts, self.off:self.off + cols]
        self.off += cols
        return Tile(v, name)


def host_consts():
    c = np.zeros((128, 520), np.float32)
    c[:, 0:128] = np.eye(128, dtype=np.float32)
    s = np.arange(128)[:, None]
    t = np.arange(128)[None, :]
    c[:, 128:256] = ((s // 32 == t // 32) & (s <= t)).astype(np.float32)
    c[:, 256:384] = (s // 32 == t // 32).astype(np.float32)
    c[:, 384:512] = (s <= t).astype(np.float32)
    c[:, 512:516] = (s // 32 == np.arange(4)[None, :]).astype(np.float32)
    inv_freq = (10000.0 ** (-np.arange(0, 32, 2, dtype=np.float32) / 32)).astype(np.float32)
    c[:, 516] = inv_freq[(np.arange(128) % 32) % 16]
    return c


def build(stop_after=None, debug=False):
    nc = bass.Bass("TRN2", target_bir_lowering=False)
    P = Prog(n_dma_sems=int(os.environ.get('KNS', 2)))
    dbgset = set(debug) if debug else set()

    def din(name, shape, dt=F32):
        return nc.dram_tensor(name, list(shape), dt, kind="ExternalInput").ap()

    def dscr(name, shape, dt):
        if name in dbgset:
            return nc.dram_tensor(name, list(shape), dt, kind="ExternalOutput").ap()
        return nc.dram_tensor(name, list(shape), dt).ap()

    x_d = din("x", [S, D])
    p_d = din("p", [2, S, 256])
    pos_d = din("pos", [1, S], I32)
    cst_d = din("cst", [128, 520])
    ln_in_g = din("ln_in_g", [D]); ln_in_b = din("ln_in_b", [D])
    w_in = din("w_in", [2, D, 2208])
    lb_logits = din("hgrn_lb_logits", [2, 256]); hg_ng = din("hgrn_norm_g", [2, 256])
    sgu_g = din("sgu_ln_g", [2, 256]); sgu_b = din("sgu_ln_b", [2, 256])
    sgu_w = din("sgu_w_s", [2, 4, 128, 128]); sgu_bs = din("sgu_b_s", [2, 4, 128])
    qng = din("mla_q_norm_g", [2, 384]); w_uq = din("mla_w_uq", [2, 384, 768])
    kvng = din("mla_kv_norm_g", [2, 256]); w_ukv = din("mla_w_ukv", [2, 256, 1024])
    w_out = din("w_out", [2, D, D])
    ln1_g = din("ln1_g", [2, D]); ln1_b = din("ln1_b", [2, D])
    w_gu = din("w_gate_up", [2, D, 2 * DFF]); w_dn = din("w_down", [2, DFF, D])
    w_pg = din("ple_w_gate", [2, D, D]); w_pp = din("ple_w_proj", [2, 256, D])
    ln2_g = din("ln2_g", [2, D]); ln2_b = din("ln2_b", [2, D])
    out_d = nc.dram_tensor("out", [S, D], F32, kind="ExternalOutput").ap()

    cosS = dscr("cosS", [128, S], F32); sinS = dscr("sinS", [128, S], F32)
    hT16 = dscr("hT16", [D, S], BF16); hres = dscr("hres", [D, S], F32)
    projTM = dscr("projTM", [S, 1536], F32)
    qnT = dscr("qnT", [512, S], BF16); qrT = dscr("qrT", [256, S], BF16)
    knT = dscr("knT", [512, S], BF16); krT = dscr("krT", [32, S], BF16)
    Vs = dscr("Vs", [S, 512], BF16)
    mixT = dscr("mixT", [D, S], BF16)
    Win16 = [dscr("Win16_%d" % l, [128, 8 * 2240], BF16) for l in range(2)]
    Wuq16 = [dscr("Wuq16_%d" % l, [128, 3 * 1024], BF16) for l in range(2)]
    Wukv16 = [dscr("Wukv16_%d" % l, [128, 2 * 1024], BF16) for l in range(2)]
    Wout16 = [dscr("Wout16_%d" % l, [128, 2 * 8 * 512], BF16) for l in range(2)]
    Wpg16 = [dscr("Wpg16_%d" % l, [128, 2 * 8 * 512], BF16) for l in range(2)]
    Wpp16 = [dscr("Wpp16_%d" % l, [128, 2 * 1024], BF16) for l in range(2)]
    Wgu16 = [dscr("Wgu16_%d" % l, [128, 11 * 2 * 8 * 256], BF16) for l in range(2)]
    Wd16 = [dscr("Wd16_%d" % l, [128, NJ * 1024], BF16) for l in range(2)]

    N16 = 69120
    N32 = 14848
    st = contextlib.ExitStack()
    st.enter_context(nc.allow_non_contiguous_dma("small strided parameter loads"))
    a16t = st.enter_context(nc.sbuf_tensor("A16", [128, N16], BF16))
    a32t = st.enter_context(nc.sbuf_tensor("A32", [128, N32], F32))
    i32t = st.enter_context(nc.sbuf_tensor("I32T", [128, 512], I32))
    A16 = Arena(a16t, N16)
    A32 = Arena(a32t, N32)
    I32T = Tile(i32t[:, :], "i32")
    PSB = [Tile(st.enter_context(nc.psum_tensor("ps%d" % i, [128, 512], F32))[:, :], "ps%d" % i) for i in range(7)]
    PS16 = Tile(st.enter_context(nc.psum_tensor("ps16", [128, 1024], BF16))[:, :], "ps16")
    bank_i = [0]

    def bank():
        b = PSB[bank_i[0] % 7]
        bank_i[0] += 1
        return b

    def dma(out, in_, reads=(), writes=(), q="sp"):
        P.dma(q, lambda e: e.dma_start(out=out, in_=in_), reads=[t.r for t in reads], writes=[t.r for t in writes])

    def op(eng, fn, reads=(), writes=()):
        P.op(eng, fn, reads=[t.r for t in reads], writes=[t.r for t in writes])

    def mm_group(out_ap, pairs, reads, writes, **kw):
        n = len(pairs)

        def fn(e):
            ins = None
            for i, (l, r) in enumerate(pairs):
                ins = e.matmul(out_ap, lhsT=l, rhs=r, start=(i == 0), stop=(i == n - 1), **kw)
            return ins
        op("pe", fn, reads, writes)

    def tcopy(eng, out_ap, in_ap, reads, writes):
        if eng == "act":
            op("act", lambda e: e.copy(out=out_ap, in_=in_ap), reads, writes)
        else:
            op(eng, lambda e: e.tensor_copy(out=out_ap, in_=in_ap), reads, writes)

    def tt(eng, out_ap, a, b, o, reads, writes):
        op(eng, lambda e: e.tensor_tensor(out=out_ap, in0=a, in1=b, op=o), reads, writes)

    def ts(eng, out_ap, a, s1, s2, o0, o1, reads, writes):
        if o1 is None:
            op(eng, lambda e: e.tensor_scalar(out=out_ap, in0=a, scalar1=s1, scalar2=None, op0=o0), reads, writes)
        else:
            op(eng, lambda e: e.tensor_scalar(out=out_ap, in0=a, scalar1=s1, scalar2=s2, op0=o0, op1=o1), reads, writes)

    def stt(out_ap, a, sc, b, o0, o1, reads, writes):
        op("dve", lambda e: e.scalar_tensor_tensor(out=out_ap, in0=a, scalar=sc, in1=b, op0=o0, op1=o1), reads, writes)

    def act(out_ap, in_ap, func, reads, writes, **kw):
        op("act", lambda e: e.activation(out=out_ap, in_=in_ap, func=func, **kw), reads, writes)

    CST = A32.alloc(520, name="cst")
    dma(CST.ap, cst_d, writes=[CST])
    ident32 = CST.ap[:, 0:128]
    hgmask = CST.ap[:, 128:256]
    lfull = CST.ap[:, 256:384]
    umask = CST.ap[:, 384:512]
    cmask = CST.ap[:, 512:516]
    invf = CST.ap[:, 516:517]
    C16 = A16.alloc(128 + 128 + 128, name="c16")
    ident16 = C16.ap[:, 0:128]
    umask16 = C16.ap[:, 128:256]
    ones16 = C16.ap[:, 256:384]
    tcopy("dve", ident16, ident32, [CST], [C16])
    tcopy("dve", umask16, umask, [CST], [C16])
    op("pool", lambda e: e.memset(ones16, 1.0), [], [C16])
    EPS = A32.alloc(2, name="eps")
    op("pool", lambda e: e.memset(EPS.ap[:, 0:1], LN_EPS), [], [EPS])
    op("pool", lambda e: e.memset(EPS.ap[:, 1:2], RMS_EPS), [], [EPS])
    eps_ln = EPS.ap[:, 0:1]
    eps_rms = EPS.ap[:, 1:2]
    DUM = A32.alloc(5 * 32, name="dummy")
    P.dummy = {"dve": DUM.ap[:, 0:32], "pool": DUM.ap[:, 32:64], "act": DUM.ap[:, 64:96], "act_src": CST.ap[:, 0:32]}
    A16.freeze()
    A32.freeze()

    def dump(name, tile, cols, parts=128):
        if ("d_" + name) not in dbgset:
            return
        dt_ = nc.dram_tensor("d_" + name, [parts, cols], F32, kind="ExternalOutput").ap()
        dma(dt_, tile.ap[0:parts, 0:cols], reads=[tile])

    def phase_begin():
        P.barrier()
        A16.reset()
        A32.reset()

    stopped = [False]

    def phase_end(name):
        if stop_after == name:
            stopped[0] = True

    def ln_fm(y, o32, o16, gcol, bcol, gb_t, tmp):
        y3 = y.v3(8); o323 = o32.v3(8); o163 = o16.v3(8)
        y16, ysq16, mean, rstd, nmr, t1 = tmp
        y163 = y16.v3(8); ysq3 = ysq16.v3(8)
        tcopy("pool", y16.ap, y.ap, [y], [y16])
        act(ysq16.ap, y.ap, AF.Square, [y], [ysq16])
        b1 = bank(); b2 = bank()
        mm_group(b1.ap, [(ones16, y163[:, kc, :]) for kc in range(8)], [C16, y16], [b1])
        mm_group(b2.ap, [(ones16, ysq3[:, kc, :]) for kc in range(8)], [C16, ysq16], [b2])
        ts("dve", mean.ap, b1.ap, 1.0 / D, None, ALU.mult, None, [b1], [mean])
        tt("dve", t1.ap, mean.ap, mean.ap, ALU.mult, [mean], [t1])
        stt(t1.ap, b2.ap, 1.0 / D, t1.ap, ALU.mult, ALU.subtract, [b2, t1], [t1])
        act(t1.ap, t1.ap, AF.Ln, [t1, EPS], [t1], bias=eps_ln, scale=1.0)
        act(rstd.ap, t1.ap, AF.Exp, [t1], [rstd], scale=-0.5)
        stt(nmr.ap, mean.ap, -1.0, rstd.ap, ALU.mult, ALU.mult, [mean, rstd], [nmr])
        rb = rstd.ap.unsqueeze(1).to_broadcast([128, 8, 512])
        nb_ = nmr.ap.unsqueeze(1).to_broadcast([128, 8, 512])
        gbc = gcol.unsqueeze(2).to_broadcast([128, 8, 512])
        bbc = bcol.unsqueeze(2).to_broadcast([128, 8, 512])
        tt("dve", o323, y3, rb, ALU.mult, [y, rstd], [o32])
        tt("pool", o323, o323, nb_, ALU.add, [o32, nmr], [o32])
        tt("dve", o323, o323, gbc, ALU.mult, [o32, gb_t], [o32])
        tt("dve", o323, o323, bbc, ALU.add, [o32, gb_t], [o32])
        tcopy("act", o163, o323, [o32], [o16])

    def ln_tmp():
        return (A16.alloc(4096, name="y16"), A16.alloc(4096, name="ysq16"), A32.alloc(512, name="mean"),
                A32.alloc(512, name="rstd"), A32.alloc(512, name="nmr"), A32.alloc(512, name="t1"))

    def load_fm_cols(vecs, name):
        t = A32.alloc(8 * len(vecs), name=name)
        for i, v in enumerate(vecs):
            dma(t.ap[:, 8 * i:8 * i + 8], v.rearrange("(kc p) -> p kc", p=128), writes=[t])
        return t

    hT16v = hT16.rearrange("(kc p) t -> p kc t", p=128)
    hresv = hres.rearrange("(kc p) t -> p kc t", p=128)
    mixTv = mixT.rearrange("(kc p) t -> p kc t", p=128)

    def phase_rope():
        phase_begin()
        posf = A32.alloc(512, name="posf"); ang = A32.alloc(512, name="ang"); kf = A32.alloc(512, name="kf")
        r_ = A32.alloc(512, name="r"); res_ = [A32.alloc(512, name="res%d" % i) for i in range(2)]
        for c in range(8):
            cs = slice(c * 512, (c + 1) * 512)
            dma(I32T.ap, pos_d[:, cs].partition_broadcast(128), writes=[I32T])
            tcopy("dve", posf.ap, I32T.ap, [I32T], [posf])
            ts("dve", ang.ap, posf.ap, invf, None, ALU.mult, None, [posf, CST], [ang])
            for which, (shift, dst) in enumerate(((0.0, sinS), (float(np.pi / 2), cosS))):
                src = ang
                if shift != 0.0:
                    ts("dve", r_.ap, ang.ap, shift, None, ALU.add, None, [ang], [r_])
                    src = r_
                ts("dve", I32T.ap, src.ap, float(1.0 / TWO_PI), None, ALU.mult, None, [src], [I32T])
                tcopy("dve", kf.ap, I32T.ap, [I32T], [kf])
                stt(r_.ap, kf.ap, -C1, src.ap, ALU.mult, ALU.add, [kf, src], [r_])
                stt(r_.ap, kf.ap, -C2, r_.ap, ALU.mult, ALU.add, [kf, r_], [r_])
                ts("dve", r_.ap, r_.ap, 3.1415925, -3.1415925, ALU.min, ALU.max, [r_], [r_])
                rr = res_[which]
                act(rr.ap, r_.ap, AF.Sin, [r_], [rr])
                dma(dst[:, cs], rr.ap, reads=[rr])
        phase_end("rope")

    def phase_ln_in():
        phase_begin()
        gb = load_fm_cols([ln_in_g, ln_in_b], "gb_in")
        xt = [A32.alloc(1024, name="xt%d" % i) for i in range(2)]
        y = A32.alloc(4096, name="y"); o16 = A16.alloc(4096, name="o16")
        tmp = ln_tmp()
        for blk in range(NB):
            for sub in range(4):
                t = blk * 4 + sub
                xb = xt[t % 2]
                dma(xb.ap, x_d[t * 128:(t + 1) * 128, :], writes=[xb])
                for half in range(2):
                    b = bank()
                    for q in range(4):
                        kc = half * 4 + q
                        op("pe", lambda e, b=b, q=q, kc=kc, xb=xb: e.transpose(out=b.ap[:, q * 128:(q + 1) * 128],
                                                                            in_=xb.ap[:, kc * 128:(kc + 1) * 128],
                                                                            identity=ident32), [xb, CST], [b])
                    dst = y.v3(8)[:, half * 4:(half + 1) * 4, sub * 128:(sub + 1) * 128]
                    tcopy("act" if half else "dve", dst, b.ap.rearrange("p (q t) -> p q t", q=4), [b], [y])
            ln_fm(y, y, o16, gb.ap[:, 0:8], gb.ap[:, 8:16], gb, tmp)
            cs = slice(blk * 512, (blk + 1) * 512)
            dma(hresv[:, :, cs], y.v3(8), reads=[y])
            dma(hT16v[:, :, cs], o16.v3(8), reads=[o16])
        phase_end("ln_in")

    def phase_weights(l):
        phase_begin()
        stg = [A32.alloc(5632, name="wst%d" % i) for i in range(2)]
        o16 = [A16.alloc(5632, name="wo%d" % i) for i in range(2)]
        cnt = [0]
        gq = A32.alloc(3, name="gq"); gkv = A32.alloc(2, name="gkv")
        dma(gq.ap, qng[l].rearrange("(m p) -> p m", p=128), writes=[gq])
        dma(gkv.ap, kvng[l].rearrange("(m p) -> p m", p=128), writes=[gkv])
        engs = ["dve", "pool", "act"]

        def nxt():
            i = cnt[0]
            cnt[0] += 1
            return stg[i % 2], o16[i % 2], engs[i % 3]

        wsel = os.environ.get("KW_SEL", "in,uq,ukv,out,pp,gu,dn").split(",")
        Wv = Win16[l].rearrange("p (kc c) -> p kc c", kc=8)
        for kc in range(8 if "in" in wsel else 0):
            s_, o_, en = nxt()
            dma(s_.ap[:, 0:2208], w_in[l, kc * 128:(kc + 1) * 128, :], writes=[s_])
            tcopy(en, o_.ap[:, 0:2208], s_.ap[:, 0:2208], [s_], [o_])
            ts("dve", o_.ap[:, 2208:2224], s_.ap[:, 2192:2208], -1.0, None, ALU.mult, None, [s_], [o_])
            tcopy("dve", o_.ap[:, 2224:2240], s_.ap[:, 2176:2192], [s_], [o_])
            dma(Wv[:, kc, :], o_.ap[:, 0:2240], reads=[o_])
        Wv = Wuq16[l].rearrange("p (m c) -> p m c", m=3)
        for m in range(3 if 'uq' in wsel else 0):
            s_, o_, en = nxt()
            dma(s_.ap[:, 0:768], w_uq[l, m * 128:(m + 1) * 128, :], writes=[s_])
            ts("dve", s_.ap[:, 0:768], s_.ap[:, 0:768], gq.ap[:, m:m + 1], float(96.0 ** -0.5), ALU.mult, ALU.mult,
               [s_, gq], [s_])
            s3 = s_.ap[:, 0:768].rearrange("p (h e) -> p h e", e=96)
            tcopy("dve", o_.ap[:, 0:512].rearrange("p (h d) -> p h d", d=64), s3[:, :, 0:64], [s_], [o_])
            tcopy("pool", o_.ap[:, 512:768].rearrange("p (h r) -> p h r", r=32), s3[:, :, 64:96], [s_], [o_])
            rot = o_.ap[:, 768:1024].rearrange("p (h r) -> p h r", r=32)
            ts("dve", rot[:, :, 0:16], s3[:, :, 80:96], -1.0, None, ALU.mult, None, [s_], [o_])
            tcopy("pool", rot[:, :, 16:32], s3[:, :, 64:80], [s_], [o_])
            dma(Wv[:, m, :], o_.ap[:, 0:1024], reads=[o_])
        Wv = Wukv16[l].rearrange("p (m c) -> p m c", m=2)
        for m in range(2 if 'ukv' in wsel else 0):
            s_, o_, en = nxt()
            dma(s_.ap[:, 0:1024], w_ukv[l, m * 128:(m + 1) * 128, :], writes=[s_])
            ts("dve", s_.ap[:, 0:1024], s_.ap[:, 0:1024], gkv.ap[:, m:m + 1], None, ALU.mult, None, [s_, gkv], [s_])
            s3 = s_.ap[:, 0:1024].rearrange("p (h e) -> p h e", e=128)
            tcopy("dve", o_.ap[:, 0:512].rearrange("p (h d) -> p h d", d=64), s3[:, :, 0:64], [s_], [o_])
            tcopy("pool", o_.ap[:, 512:1024].rearrange("p (h d) -> p h d", d=64), s3[:, :, 64:128], [s_], [o_])
            dma(Wv[:, m, :], o_.ap[:, 0:1024], reads=[o_])
        for src, dst in ((w_out, Wout16), (w_pg, Wpg16)):
            Wv = dst[l].rearrange("p (dq kc c) -> p dq kc c", dq=2, kc=8)
            for kc in range(8 if 'out' in wsel else 0):
                s_, o_, en = nxt()
                dma(s_.ap[:, 0:1024], src[l, kc * 128:(kc + 1) * 128, :], writes=[s_])
                tcopy(en, o_.ap[:, 0:1024], s_.ap[:, 0:1024], [s_], [o_])
                dma(Wv[:, :, kc, :], o_.ap[:, 0:1024].rearrange("p (dq c) -> p dq c", dq=2), reads=[o_])
        Wv = Wpp16[l].rearrange("p (m c) -> p m c", m=2)
        for m in range(2 if 'pp' in wsel else 0):
            s_, o_, en = nxt()
            dma(s_.ap[:, 0:1024], w_pp[l, m * 128:(m + 1) * 128, :], writes=[s_])
            tcopy(en, o_.ap[:, 0:1024], s_.ap[:, 0:1024], [s_], [o_])
            dma(Wv[:, m, :], o_.ap[:, 0:1024], reads=[o_])
        Wv = Wgu16[l].rearrange("p (jg gu kc c) -> p gu jg kc c", jg=11, gu=2, kc=8)
        for kc in range(8 if 'gu' in wsel else 0):
            s_, o_, en = nxt()
            dma(s_.ap[:, 0:5632], w_gu[l, kc * 128:(kc + 1) * 128, :], writes=[s_])
            tcopy(en, o_.ap[:, 0:2816], s_.ap[:, 0:2816], [s_], [o_])
            tcopy(engs[(cnt[0] + 1) % 3], o_.ap[:, 2816:5632], s_.ap[:, 2816:5632], [s_], [o_])
            for gu in range(2):
                dma(Wv[:, gu, :, kc, :], o_.ap[:, gu * 2816:(gu + 1) * 2816].rearrange("p (jg c) -> p jg c", jg=11),
                    reads=[o_])
        Wv = Wd16[l].rearrange("p (j c) -> p j c", j=NJ)
        for j in range(int(os.environ.get('KDN', NJ)) if 'dn' in wsel else 0):
            s_, o_, en = nxt()
            dma(s_.ap[:, 0:1024], w_dn[l, j * 128:(j + 1) * 128, :], writes=[s_])
            tcopy(en, o_.ap[:, 0:1024], s_.ap[:, 0:1024], [s_], [o_])
            dma(Wv[:, j, :], o_.ap[:, 0:1024], reads=[o_])
        phase_end("weights%d" % l)

    def phase_inproj(l):
        phase_begin()
        Win = A16.alloc(8 * 2240, name="Win"); Wuq = A16.alloc(3 * 1024, name="Wuq"); Wukv = A16.alloc(2 * 1024, name="Wukv")
        dma(Win.ap, Win16[l], writes=[Win]); dma(Wuq.ap, Wuq16[l], writes=[Wuq]); dma(Wukv.ap, Wukv16[l], writes=[Wukv])
        Win3 = Win.v3(8); Wuq3 = Wuq.v3(3); Wukv3 = Wukv.v3(2)
        hTb = [A16.alloc(4096, name="hT%d" % i) for i in range(2)]
        csb = [A32.alloc(1024, name="cs%d" % i) for i in range(2)]
        cq32 = A32.alloc(1536, name="cq32"); rstd = A32.alloc(512, name="rstdq")
        sq16 = A16.alloc(1536, name="sq16"); cn16 = A16.alloc(1536, name="cn16")
        qn16 = A16.alloc(2048, name="qn16"); qr16 = A16.alloc(1024, name="qr16"); kn16 = A16.alloc(2048, name="kn16")
        kr16 = A16.alloc(512, name="kr16")
        v16 = [A16.alloc(512, name="v16_%d" % i) for i in range(2)]
        t1 = A32.alloc(512, name="t1"); t2 = A32.alloc(512, name="t2")
        ptm = [A32.alloc(1536, name="ptm%d" % i) for i in range(2)]
        cq323 = cq32.v3(3); sq3 = sq16.v3(3); cn3 = cn16.v3(3)
        for blk in range(int(os.environ.get('KNB', NB))):
            cs = slice(blk * 512, (blk + 1) * 512)
            hT = hTb[blk % 2]; hT3 = hT.v3(8)
            dma(hT3, hT16v[:, :, cs], writes=[hT])
            cb = csb[blk % 2]
            dma(cb.ap[:, 0:512], cosS[:, cs], writes=[cb]); dma(cb.ap[:, 512:1024], sinS[:, cs], writes=[cb])
            cosb = cb.ap[:, 0:512]; sinb = cb.ap[:, 512:1024]

            def lowrank(col0, nchunk, rank, W3, eps):
                for m in range(nchunk):
                    b = bank()
                    mm_group(b.ap, [(Win3[:, kc, col0 + m * 128:col0 + (m + 1) * 128], hT3[:, kc, :]) for kc in range(8)],
                             [Win, hT], [b])
                    tcopy("dve", cq323[:, m, :], b.ap, [b], [cq32])
                    act(sq3[:, m, :], cq323[:, m, :], AF.Square, [cq32], [sq16])
                lr = int(os.environ.get('KLR', 9))
                if lr < 2:
                    return
                b = bank()
                mm_group(b.ap, [(ones16, sq3[:, m, :]) for m in range(nchunk)], [C16, sq16], [b])
                if lr < 3:
                    return
                act(rstd.ap, b.ap, AF.Ln, [b, EPS], [rstd], bias=eps_rms, scale=1.0 / rank)
                act(rstd.ap, rstd.ap, AF.Exp, [rstd], [rstd], scale=-0.5)
                if lr < 4:
                    return
                for m in range(nchunk):
                    tt("dve" if m % 2 == 0 else "pool", cn3[:, m, :], cq323[:, m, :], rstd.ap, ALU.mult, [cq32, rstd], [cn16])

            def rope_fm(braw, brot, out_ap, parts, wr):
                tt("dve", t1.ap[0:parts, :], braw.ap[0:parts, :], cosb[0:parts, :], ALU.mult, [braw, cb], [t1])
                tt("dve", t2.ap[0:parts, :], brot.ap[0:parts, :], sinb[0:parts, :], ALU.mult, [brot, cb], [t2])
                tt("pool", out_ap, t1.ap[0:parts, :], t2.ap[0:parts, :], ALU.add, [t1, t2], [wr])

            parts = os.environ.get('KPARTS', 'q,kv,kr,tm').split(',')
            if 'q' in parts:
              lowrank(1536, 3, 384.0, Wuq3, RMS_EPS)
              for mc in range(4):
                  b = bank()
                  mm_group(b.ap, [(Wuq3[:, m, mc * 128:(mc + 1) * 128], cn3[:, m, :]) for m in range(3)], [Wuq, cn16], [b])
                  tcopy("act", qn16.v3(4)[:, mc, :], b.ap, [b], [qn16])
              for rc in range(2):
                  ba = bank(); bb = bank()
                  mm_group(ba.ap, [(Wuq3[:, m, 512 + rc * 128:512 + (rc + 1) * 128], cn3[:, m, :]) for m in range(3)],
                           [Wuq, cn16], [ba])
                  mm_group(bb.ap, [(Wuq3[:, m, 768 + rc * 128:768 + (rc + 1) * 128], cn3[:, m, :]) for m in range(3)],
                           [Wuq, cn16], [bb])
                  rope_fm(ba, bb, qr16.v3(2)[:, rc, :], 128, qr16)
              dma(qnT.rearrange("(mc p) t -> p mc t", p=128)[:, :, cs], qn16.v3(4), reads=[qn16])
              dma(qrT.rearrange("(rc p) t -> p rc t", p=128)[:, :, cs], qr16.v3(2), reads=[qr16])
            if 'lr' in parts:
              lowrank(1920, 2, 256.0, Wukv3, RMS_EPS)
            if 'kv' in parts:
              lowrank(1920, 2, 256.0, Wukv3, RMS_EPS)
              for mc in range(4):
                  b = bank()
                  mm_group(b.ap, [(Wukv3[:, m, mc * 128:(mc + 1) * 128], cn3[:, m, :]) for m in range(2)], [Wukv, cn16], [b])
                  tcopy("act", kn16.v3(4)[:, mc, :], b.ap, [b], [kn16])
              dma(knT.rearrange("(mc p) t -> p mc t", p=128)[:, :, cs], kn16.v3(4), reads=[kn16])
              for sub in range(4):
                  b = bank(); vv = v16[sub % 2]
                  mm_group(b.ap, [(cn3[:, m, sub * 128:(sub + 1) * 128], Wukv3[:, m, 512:1024]) for m in range(2)],
                           [Wukv, cn16], [b])
                  tcopy("dve", vv.ap, b.ap, [b], [vv])
                  t = blk * 4 + sub
                  dma(Vs[t * 128:(t + 1) * 128, :], vv.ap, reads=[vv])
            if 'kr' in parts:
              ba = bank(); bb = bank()
              mm_group(ba.ap[0:32, :], [(Win3[:, kc, 2176:2208], hT3[:, kc, :]) for kc in range(8)], [Win, hT], [ba])
              mm_group(bb.ap[0:32, :], [(Win3[:, kc, 2208:2240], hT3[:, kc, :]) for kc in range(8)], [Win, hT], [bb])
              rope_fm(ba, bb, kr16.ap[0:32, :], 32, kr16)
              dma(krT[:, cs], kr16.ap[0:32, :], reads=[kr16])
            for sub in range(4 if 'tm' in parts else 0):
                t = blk * 4 + sub
                pt = ptm[t % 2]
                for cbk in range(3):
                    b = bank()
                    mm_group(b.ap, [(hT3[:, kc, sub * 128:(sub + 1) * 128], Win3[:, kc, cbk * 512:(cbk + 1) * 512])
                                    for kc in range(8)], [Win, hT], [b])
                    tcopy("act" if cbk == 1 else "dve", pt.ap[:, cbk * 512:(cbk + 1) * 512], b.ap, [b], [pt])
                dma(projTM[t * 128:(t + 1) * 128, :], pt.ap, reads=[pt])
        phase_end("inproj%d" % l)

    def phase_hgrn_sgu(l):
        phase_begin()
        par = A32.alloc(256 * 6, name="par")
        lbt = par.ap[:, 0:256]; omlb = par.ap[:, 256:512]; ngt = par.ap[:, 512:768]
        sgt = par.ap[:, 768:1024]; sbt = par.ap[:, 1024:1280]; ltmp = par.ap[:, 1280:1536]
        dma(ngt, hg_ng[l:l + 1, :].partition_broadcast(128), writes=[par])
        dma(sgt, sgu_g[l:l + 1, :].partition_broadcast(128), writes=[par])
        dma(sbt, sgu_b[l:l + 1, :].partition_broadcast(128), writes=[par])
        if l == 0:
            op("dve", lambda e: e.memset(lbt, 0.0), [], [par])
            op("dve", lambda e: e.memset(omlb, 1.0), [], [par])
        else:
            dma(lbt, lb_logits[0:1, :].partition_broadcast(128), writes=[par])
            dma(ltmp, lb_logits[1:2, :].partition_broadcast(128), writes=[par])
            tt("dve", lbt, lbt, ltmp, ALU.subtract, [par], [par])
            act(lbt, lbt, AF.Exp, [par], [par])
            ts("dve", lbt, lbt, 1.0, None, ALU.add, None, [par], [par])
            op("dve", lambda e: e.reciprocal(out=lbt, in_=lbt), [par], [par])
            ts("dve", omlb, lbt, -1.0, 1.0, ALU.mult, ALU.add, [par], [par])
        bscol = A32.alloc(4, name="bscol")
        dma(bscol.ap, sgu_bs[l].rearrange("g t -> t g"), writes=[bscol])
        WsT = A32.alloc(512, name="WsT"); wtmp = A32.alloc(512, name="wtmp")
        dma(wtmp.v3(4), sgu_w[l].rearrange("g t s -> t g s"), writes=[wtmp])
        b = bank()
        for g in range(4):
            op("pe", lambda e, g=g, b=b: e.transpose(out=b.ap[:, g * 128:(g + 1) * 128], in_=wtmp.ap[:, g * 128:(g + 1) * 128],
                                                  identity=ident32), [wtmp, CST], [b])
        tt("dve", WsT.v3(4), b.ap.rearrange("p (g t) -> p g t", g=4), umask.unsqueeze(1).to_broadcast([128, 4, 128]),
           ALU.mult, [b, CST], [WsT])
        Sh = A32.alloc(5 * 128, name="Sh")
        Sh4 = Sh.v4(5, 2)
        op("pool", lambda e: e.memset(Sh.ap, 0.0), [], [Sh])
        X = [A32.alloc(1536, name="X%d" % i) for i in range(2)]
        GU = A32.alloc(512, name="GU"); E = A32.alloc(768, name="E")
        f_ = A32.alloc(256, name="f"); logf = A32.alloc(256, name="logf"); omf = A32.alloc(256, name="omf")
        qs = A32.alloc(256, name="qs"); gs = A32.alloc(256, name="gs")
        eb = A32.alloc(768, name="eb")
        QK = A32.alloc(512, name="QK")
        Ke = A32.alloc(256, name="Ke"); Vexp = A32.alloc(1024, name="Vexp")
        QKT = A32.alloc(512, name="QKT")
        PTm = A32.alloc(512, name="PTm"); dcol = A32.alloc(8, name="dcol"); T2 = A32.alloc(640, name="T2")
        oi = A32.alloc(256, name="oi"); osq = A32.alloc(256, name="osq"); st4 = A32.alloc(4, name="st4")
        vn = A32.alloc(256, name="vn"); bst = A32.alloc(8, name="bst"); mv = A32.alloc(2, name="mv"); rs1 = A32.alloc(1, name="rs1")
        O16 = [A16.alloc(512, name="O16_%d" % i) for i in range(2)]
        MT = [A16.alloc(2048, name="MT%d" % i) for i in range(2)]
        for t in range(int(os.environ.get('KNT', NT))):
            Xt = X[t % 2]
            dma(Xt.ap, projTM[t * 128:(t + 1) * 128, :], writes=[Xt])
            xq = Xt.ap[:, 0:256]; xf = Xt.ap[:, 256:512]; xi = Xt.ap[:, 512:768]; xg = Xt.ap[:, 768:1024]
            O = O16[t % 2]
            act(GU.ap, Xt.ap[:, 1024:1536], AF.Gelu, [Xt], [GU])
            op("dve", lambda e: e.tensor_reduce(out=bst.ap[:, 0:1], in_=GU.ap[:, 256:512].unsqueeze(1), axis=AX.X, op=ALU.add),
               [GU], [bst])
            tt("dve", vn.ap, GU.ap[:, 256:512], GU.ap[:, 256:512], ALU.mult, [GU], [vn])
            op("dve", lambda e: e.tensor_reduce(out=bst.ap[:, 1:2], in_=vn.ap.unsqueeze(1), axis=AX.X, op=ALU.add), [vn], [bst])
            ts("dve", mv.ap[:, 0:1], bst.ap[:, 0:1], 1.0 / 256, None, ALU.mult, None, [bst], [mv])
            tt("dve", bst.ap[:, 2:3], mv.ap[:, 0:1], mv.ap[:, 0:1], ALU.mult, [mv], [bst])
            stt(mv.ap[:, 1:2], bst.ap[:, 1:2], 1.0 / 256, bst.ap[:, 2:3], ALU.mult, ALU.subtract, [bst], [mv])
            act(rs1.ap, mv.ap[:, 1:2], AF.Ln, [mv, EPS], [rs1], bias=eps_ln, scale=1.0)
            act(rs1.ap, rs1.ap, AF.Exp, [rs1], [rs1], scale=-0.5)
            ts("dve", vn.ap, GU.ap[:, 256:512], mv.ap[:, 0:1], rs1.ap[:, 0:1], ALU.subtract, ALU.mult, [GU, mv, rs1], [vn])
            tt("pool", vn.ap, vn.ap, sgt, ALU.mult, [vn, par], [vn])
            tt("pool", vn.ap, vn.ap, sbt, ALU.add, [vn, par], [vn])
            bz = bank()

            def zfn(e, bz=bz):
                ins = None
                for g in range(4):
                    ins = e.matmul(bz.ap[:, g * 64:(g + 1) * 64], lhsT=WsT.ap[:, g * 128:(g + 1) * 128],
                                   rhs=vn.ap[:, g * 64:(g + 1) * 64], start=True, stop=True)
                return ins
            op("pe", zfn, [WsT, vn], [bz])
            for g in range(4):
                stt(O.ap[:, 256 + g * 64:256 + (g + 1) * 64], bz.ap[:, g * 64:(g + 1) * 64], bscol.ap[:, g:g + 1],
                    GU.ap[:, g * 64:(g + 1) * 64], ALU.add, ALU.mult, [bz, bscol, GU], [O])
            act(E.ap[:, 0:512], Xt.ap[:, 0:512], AF.Exp, [Xt], [E], scale=-1.0)
            act(E.ap[:, 512:768], xg, AF.Exp, [Xt], [E], scale=-1.0)
            ts("dve", E.ap, E.ap, 1.0, None, ALU.add, None, [E], [E])
            op("dve", lambda e: e.reciprocal(out=E.ap, in_=E.ap), [E], [E])
            tt("pool", qs.ap, xq, E.ap[:, 0:256], ALU.mult, [Xt, E], [qs])
            tt("pool", gs.ap, xg, E.ap[:, 512:768], ALU.mult, [Xt, E], [gs])
            tt("dve", f_.ap, E.ap[:, 256:512], omlb, ALU.mult, [E, par], [f_])
            tt("dve", f_.ap, f_.ap, lbt, ALU.add, [f_, par], [f_])
            act(logf.ap, f_.ap, AF.Ln, [f_], [logf])
            ts("pool", omf.ap, f_.ap, -1.0, 1.0, ALU.mult, ALU.add, [f_], [omf])
            bb_ = bank()
            op("pe", lambda e, bb_=bb_: e.matmul(bb_.ap[:, 0:256], lhsT=hgmask, rhs=logf.ap, start=True, stop=True),
               [CST, logf], [bb_])
            op("pe", lambda e, bb_=bb_: e.matmul(bb_.ap[:, 256:512], lhsT=lfull, rhs=logf.ap, start=True, stop=True),
               [CST, logf], [bb_])
            bd = bank()

            def dfn(e, bd=bd):
                ins = None
                for j in range(2):
                    ins = e.matmul(bd.ap[:, j * 4:(j + 1) * 4], lhsT=logf.ap[:, j * 128:(j + 1) * 128], rhs=cmask,
                                   start=True, stop=True)
                return ins
            op("pe", dfn, [logf, CST], [bd])
            act(eb.ap[:, 0:256], bb_.ap[:, 0:256], AF.Exp, [bb_], [eb])
            act(eb.ap[:, 256:512], bb_.ap[:, 0:256], AF.Exp, [bb_], [eb], scale=-1.0)
            act(eb.ap[:, 512:768], bb_.ap[:, 256:512], AF.Exp, [bb_], [eb])
            act(dcol.ap, bd.ap[:, 0:8], AF.Exp, [bd], [dcol])
            tt("dve", QK.ap[:, 0:256], qs.ap, eb.ap[:, 0:256], ALU.mult, [qs, eb], [QK])
            tt("dve", QK.ap[:, 256:512], omf.ap, eb.ap[:, 256:512], ALU.mult, [omf, eb], [QK])
            tt("dve", Ke.ap, QK.ap[:, 256:512], eb.ap[:, 512:768], ALU.mult, [QK, eb], [Ke])
            tt("pool", Vexp.ap.rearrange("p (h c v) -> p h c v", h=4, c=4),
               xi.rearrange("p (h v) -> p h v", h=4).unsqueeze(2).to_broadcast([128, 4, 4, 64]),
               cmask.unsqueeze(1).unsqueeze(3).to_broadcast([128, 4, 4, 64]), ALU.mult, [Xt, CST], [Vexp])
            btr = bank()
            for i4 in range(4):
                op("pe", lambda e, i4=i4, btr=btr: e.transpose(out=btr.ap[:, i4 * 128:(i4 + 1) * 128],
                                                              in_=QK.ap[:, i4 * 128:(i4 + 1) * 128], identity=ident32),
                   [QK, CST], [btr])
            tcopy("act", QKT.ap, btr.ap, [btr], [QKT])
            bsc = [bank(), bank()]

            def scfn(e, bsc=bsc):
                ins = None
                for h in range(4):
                    j, ee = h // 2, h % 2
                    ins = e.matmul(bsc[ee].ap[:, j * 128:(j + 1) * 128],
                                   lhsT=QKT.ap[ee * 64:(ee + 1) * 64, 256 + j * 128:256 + (j + 1) * 128],
                                   rhs=QKT.ap[ee * 64:(ee + 1) * 64, j * 128:(j + 1) * 128], start=True, stop=True)
                return ins
            op("pe", scfn, [QKT], bsc)
            for h in range(4):
                j, ee = h // 2, h % 2
                tt("dve", PTm.ap[:, h * 128:(h + 1) * 128], bsc[ee].ap[:, j * 128:(j + 1) * 128], hgmask, ALU.mult,
                   [bsc[ee], CST], [PTm])
            bkv = [bank(), bank()]

            def kvfn(e, bkv=bkv):
                ins = None
                for h in range(4):
                    j = h // 2
                    ins = e.matmul(bkv[h // 2].ap[:, (h % 2) * 256:(h % 2 + 1) * 256], lhsT=Ke.ap[:, j * 128:(j + 1) * 128],
                                   rhs=Vexp.ap[:, h * 256:(h + 1) * 256], start=True, stop=True)
                return ins
            op("pe", kvfn, [Ke, Vexp], bkv)
            for c in range(4):
                for h in range(4):
                    j, ee = h // 2, h % 2
                    rows = slice(ee * 64, (ee + 1) * 64)
                    stt(Sh4[rows, c + 1, j, :], Sh4[rows, c, j, :], dcol.ap[rows, j * 4 + c:j * 4 + c + 1],
                        bkv[j].ap[rows, ee * 256 + c * 64:ee * 256 + (c + 1) * 64], ALU.mult, ALU.add,
                        [Sh, dcol, bkv[j]], [Sh])
            bo = [bank(), bank()]
            boi = bank()

            def ofn(e, bo=bo, boi=boi, xi=xi):
                ins = None
                for h in range(4):
                    ins = e.matmul(boi.ap[:, h * 64:(h + 1) * 64], lhsT=PTm.ap[:, h * 128:(h + 1) * 128],
                                   rhs=xi[:, h * 64:(h + 1) * 64], start=True, stop=True)
                for h in range(4):
                    j, ee = h // 2, h % 2
                    for c in range(4):
                        ins = e.matmul(bo[ee].ap[32 * c:32 * c + 32, j * 64:(j + 1) * 64],
                                       lhsT=QKT.ap[ee * 64:(ee + 1) * 64, j * 128 + 32 * c:j * 128 + 32 * c + 32],
                                       rhs=Sh4[ee * 64:(ee + 1) * 64, c, j, :], start=True, stop=True,
                                       tile_position=(ee * 64, 32 * c))
                return ins
            op("pe", ofn, [PTm, Xt, QKT, Sh], bo + [boi])
            tcopy("act", T2.ap[:, 0:128], bo[0].ap[:, 0:128], [bo[0]], [T2])
            tcopy("act", T2.ap[:, 256:512], boi.ap[:, 0:256], [boi], [T2])
            tcopy("act", T2.ap[:, 512:640], bo[1].ap[:, 0:128], [bo[1]], [T2])
            for h in range(4):
                j, ee = h // 2, h % 2
                src = T2.ap[:, j * 64:(j + 1) * 64] if ee == 0 else T2.ap[:, 512 + j * 64:512 + (j + 1) * 64]
                tt("dve", oi.ap[:, h * 64:(h + 1) * 64], T2.ap[:, 256 + h * 64:256 + (h + 1) * 64], src, ALU.add, [T2], [oi])
            tcopy("pool", Sh4[:, 0, :, :], Sh4[:, 4, :, :], [Sh], [Sh])
            tt("dve", osq.ap, oi.ap, oi.ap, ALU.mult, [oi], [osq])
            op("dve", lambda e: e.tensor_reduce(out=st4.ap, in_=osq.v3(4), axis=AX.X, op=ALU.add), [osq], [st4])
            act(st4.ap, st4.ap, AF.Ln, [st4, EPS], [st4], bias=eps_rms, scale=1.0 / 64)
            act(st4.ap, st4.ap, AF.Exp, [st4], [st4], scale=-0.5)
            tt("dve", oi.v3(4), oi.v3(4), st4.ap.unsqueeze(2).to_broadcast([128, 4, 64]), ALU.mult, [oi, st4], [oi])
            tt("pool", oi.ap, oi.ap, ngt, ALU.mult, [oi, par], [oi])
            tt("dve", O.ap[:, 0:256], oi.ap, gs.ap, ALU.mult, [oi, gs], [O])
            if t == 0:
                for nm, tl, cc in (("X", Xt, 1536), ("GU", GU, 512), ("vn", vn, 256), ("E", E, 768), ("logf", logf, 256),
                                   ("eb", eb, 768), ("QK", QK, 512), ("Ke", Ke, 256), ("PTm", PTm, 512), ("Sh", Sh, 640),
                                   ("oi", oi, 256), ("T2", T2, 640), ("dcol", dcol, 8), ("QKT", QKT, 512), ("mv", mv, 2)):
                    dump(nm, tl, cc)
            for i4 in range(4):
                op("pe", lambda e, i4=i4, O=O: e.transpose(out=PS16.ap[:, i4 * 128:(i4 + 1) * 128],
                                                          in_=O.ap[:, i4 * 128:(i4 + 1) * 128], identity=ident16),
                   [O, C16], [PS16])
            mt = MT[(t // 4) % 2]
            tcopy("act", mt.v3(4)[:, :, (t % 4) * 128:(t % 4 + 1) * 128], PS16.ap[:, 0:512].rearrange("p (c t) -> p c t", c=4),
                  [PS16], [mt])
            if t % 4 == 3:
                blk = t // 4
                dma(mixTv[:, 0:4, blk * 512:(blk + 1) * 512], mt.v3(4), reads=[mt])
        phase_end("hgrn%d" % l)

    def phase_attn(l):
        phase_begin()
        KT = A16.alloc(8 * S, parts=96, name="KT")
        KT3 = KT.v3(8)
        dma(KT3[0:64, :, :], knT.rearrange("(h d) t -> d h t", d=64), writes=[KT])
        for h in range(8):
            dma(KT3[64:96, h, :], krT, writes=[KT])
        Va = A16.alloc(32 * 8 * 65, name="Va")
        Va4 = Va.ap.rearrange("p (n h v) -> p n h v", n=32, h=8)
        op("pool", lambda e: e.memset(Va.ap, 1.0), [], [Va])
        vst = [A16.alloc(1024, name="vst%d" % i) for i in range(2)]
        for q in range(16):
            vs_ = vst[q % 2]
            dma(vs_.v3(2), Vs.rearrange("(n p) c -> p n c", p=128)[:, q * 2:(q + 1) * 2, :], writes=[vs_])
            tcopy("dve" if q % 2 == 0 else "pool", Va4[:, q * 2:(q + 1) * 2, :, 0:64],
                  vs_.ap.rearrange("p (n h v) -> p n h v", n=2, h=8), [vs_], [Va])
        Qb = [A16.alloc(8 * 512, parts=96, name="Qb%d" % i) for i in range(2)]
        PT = [A16.alloc(512, name="PT%d" % i) for i in range(4)]
        oc = [A16.alloc(512, name="oc%d" % i) for i in range(4)]
        mc = [A16.alloc(2048, name="mc%d" % i) for i in range(1)] * 2
        rinv = A32.alloc(8, name="rinv")
        accs = A32.alloc(260, name="accs")
        PVB = PSB[4:6]
        pvi = [0]
        SC = PSB[0:4]
        ACC = PSB[4:6]
        pti = 0
        sci = 0
        for blk in range(NB):
            cs = slice(blk * 512, (blk + 1) * 512)
            Q = Qb[blk % 2]; Q3 = Q.v3(8)
            dma(Q3[0:64, :, :], qnT.rearrange("(h d) t -> d h t", d=64)[:, :, cs], writes=[Q])
            dma(Q3[64:96, :, :], qrT.rearrange("(h r) t -> r h t", r=32)[:, :, cs], writes=[Q])
            for h in range(8):
                acc = ACC[h % 2]
                nkt = 4 * blk + 4
                for kt in range(nkt):
                    q0 = 0 if kt <= 4 * blk else (kt - 4 * blk) * 128
                    n = 512 - q0
                    sc = SC[sci % 4]; sci += 1
                    pt = PT[pti % 4]; pti += 1
                    op("pe", lambda e, sc=sc, h=h, kt=kt, q0=q0, n=n, Q3=Q3: e.matmul(
                        sc.ap[:, 0:n], lhsT=KT3[0:96, h, kt * 128:(kt + 1) * 128], rhs=Q3[0:96, h, q0:512],
                        start=True, stop=True), [KT, Q], [sc])
                    act(pt.ap[:, 0:n], sc.ap[:, 0:n], AF.Exp, [sc], [pt])
                    if kt >= 4 * blk:
                        tt("pool", pt.ap[:, 0:128], pt.ap[:, 0:128], umask16, ALU.mult, [pt, C16], [pt])

                    pvb = PVB[pvi[0] % 2]; pvi[0] += 1
                    qs0 = q0 // 128

                    def pvfn(e, pt=pt, pvb=pvb, h=h, kt=kt, q0=q0):
                        ins = None
                        for qsub in range(q0 // 128, 4):
                            ins = e.matmul(pvb.ap[:, qsub * 65:(qsub + 1) * 65],
                                           lhsT=pt.ap[:, qsub * 128 - q0:qsub * 128 - q0 + 128], rhs=Va4[:, kt, h, :],
                                           start=True, stop=True)
                        return ins
                    op("pe", pvfn, [pt, Va], [pvb])
                    if kt == 0:
                        tcopy("dve", accs.ap[:, 0:260], pvb.ap[:, 0:260], [pvb], [accs])
                    else:
                        tt("dve", accs.ap[:, qs0 * 65:260], accs.ap[:, qs0 * 65:260], pvb.ap[:, qs0 * 65:260], ALU.add,
                           [accs, pvb], [accs])
                a3 = accs.ap[:, 0:260].rearrange("p (q v) -> p q v", q=4)
                op("dve", lambda e, a3=a3: e.reciprocal(out=rinv.ap[:, 0:4], in_=a3[:, :, 64]), [accs], [rinv])
                for qsub in range(4):
                    ts("dve", oc[qsub].ap[:, h * 64:(h + 1) * 64], accs.ap[:, qsub * 65:qsub * 65 + 64],
                       rinv.ap[:, qsub:qsub + 1], None, ALU.mult, None, [accs, rinv], [oc[qsub]])
            for qsub in range(4):
                t = blk * 4 + qsub
                for i4 in range(4):
                    op("pe", lambda e, i4=i4, qsub=qsub: e.transpose(out=PS16.ap[:, i4 * 128:(i4 + 1) * 128],
                                                                    in_=oc[qsub].ap[:, i4 * 128:(i4 + 1) * 128],
                                                                    identity=ident16), [oc[qsub], C16], [PS16])
                m_ = mc[blk % 2]
                tcopy("act", m_.v3(4)[:, :, qsub * 128:(qsub + 1) * 128], PS16.ap[:, 0:512].rearrange("p (c t) -> p c t", c=4),
                      [PS16], [m_])
            dma(mixTv[:, 4:8, cs], mc[blk % 2].v3(4), reads=[mc[blk % 2]])
        phase_end("attn%d" % l)

    def phase_ffn(l):
        phase_begin()
        gb = load_fm_cols([ln1_g[l], ln1_b[l], ln2_g[l], ln2_b[l]], "gb")
        WR = [A16.alloc(4096, name="wr%d" % i) for i in range(3)]
        wri = [0]

        def wslot():
            w = WR[wri[0] % 3]
            wri[0] += 1
            return w
        mixb = A16.alloc(4096, name="mixb"); pT16 = A16.alloc(1024, name="pT16")
        h16 = A16.alloc(4096, name="h16"); act16 = A16.alloc(NJ * 512, name="act16")
        bufA = A32.alloc(4096, name="bufA"); bufB = A32.alloc(4096, name="bufB")
        ptile = [A32.alloc(256, name="pt%d" % i) for i in range(2)]
        sg = [A32.alloc(512, name="sg%d" % i) for i in range(2)]
        tmp = ln_tmp()
        A3 = bufA.v3(8); B3 = bufB.v3(8); h163 = h16.v3(8); mix3 = mixb.v3(8); act3 = act16.v3(NJ); pT3 = pT16.v3(2)
        Woutv = Wout16[l].rearrange("p (dq x) -> p dq x", dq=2)
        Wpgv = Wpg16[l].rearrange("p (dq x) -> p dq x", dq=2)
        Wguv = Wgu16[l].rearrange("p (jg x) -> p jg x", jg=11)
        Wdv = Wd16[l].rearrange("p (j c) -> p j c", j=NJ)
        for blk in range(NB):
            cs = slice(blk * 512, (blk + 1) * 512)
            dma(mix3, mixTv[:, :, cs], writes=[mixb])
            dma(A3, hresv[:, :, cs], writes=[bufA])
            for sub in range(4):
                t = blk * 4 + sub
                pt_ = ptile[t % 2]
                dma(pt_.ap, p_d[l, t * 128:(t + 1) * 128, :], writes=[pt_])
                b = bank()
                for m in range(2):
                    op("pe", lambda e, b=b, m=m, pt_=pt_: e.transpose(out=b.ap[:, m * 128:(m + 1) * 128],
                                                                     in_=pt_.ap[:, m * 128:(m + 1) * 128], identity=ident32),
                       [pt_, CST], [b])
                tcopy("act", pT3[:, :, sub * 128:(sub + 1) * 128], b.ap[:, 0:256].rearrange("p (m t) -> p m t", m=2), [b], [pT16])
            for dq in range(2):
                w = wslot(); w3 = w.v3(8)
                dma(w.ap, Woutv[:, dq, :], writes=[w])
                for d4 in range(4):
                    dc = dq * 4 + d4
                    b = bank()
                    mm_group(b.ap, [(w3[:, kc, d4 * 128:(d4 + 1) * 128], mix3[:, kc, :]) for kc in range(8)], [w, mixb], [b])
                    stt(B3[:, dc, :], A3[:, dc, :], ALPHA, b.ap, ALU.mult, ALU.add, [bufA, b], [bufB])
            ln_fm(bufB, bufB, h16, gb.ap[:, 0:8], gb.ap[:, 8:16], gb, tmp)
            wp = wslot(); wp3 = wp.ap[:, 0:2048].rearrange("p (m c) -> p m c", m=2)
            dma(wp.ap[:, 0:2048], Wpp16[l], writes=[wp])
            for dq in range(2):
                w = wslot(); w3 = w.v3(8)
                dma(w.ap, Wpgv[:, dq, :], writes=[w])
                for d4 in range(4):
                    dc = dq * 4 + d4
                    b = bank(); b2 = bank(); s_ = sg[dc % 2]
                    mm_group(b.ap, [(w3[:, kc, d4 * 128:(d4 + 1) * 128], h163[:, kc, :]) for kc in range(8)], [w, h16], [b])
                    mm_group(b2.ap, [(wp3[:, m, dc * 128:(dc + 1) * 128], pT3[:, m, :]) for m in range(2)], [wp, pT16], [b2])
                    act(s_.ap, b.ap, AF.Sigmoid, [b], [s_])
                    tt("dve", A3[:, dc, :], s_.ap, b2.ap, ALU.mult, [s_, b2], [bufA])
                    stt(A3[:, dc, :], B3[:, dc, :], ALPHA, A3[:, dc, :], ALU.mult, ALU.add, [bufB, bufA], [bufA])
            for jg in range(11):
                w = wslot(); w4 = w.ap.rearrange("p (gu kc c) -> p gu kc c", gu=2, kc=8)
                dma(w.ap, Wguv[:, jg, :], writes=[w])
                for jj in range(2):
                    j = jg * 2 + jj
                    bg = bank(); bu = bank(); s_ = sg[j % 2]
                    mm_group(bg.ap, [(w4[:, 0, kc, jj * 128:(jj + 1) * 128], h163[:, kc, :]) for kc in range(8)], [w, h16], [bg])
                    mm_group(bu.ap, [(w4[:, 1, kc, jj * 128:(jj + 1) * 128], h163[:, kc, :]) for kc in range(8)], [w, h16], [bu])
                    act(s_.ap, bg.ap, AF.Silu, [bg], [s_])
                    tt("dve", act3[:, j, :], s_.ap, bu.ap, ALU.mult, [s_, bu], [act16])
            for dc in range(8):
                w = wslot(); w3 = w.ap[:, 0:NJ * 128].rearrange("p (j c) -> p j c", j=NJ)
                dma(w3, Wdv[:, :, dc * 128:(dc + 1) * 128], writes=[w])
                b = bank()
                mm_group(b.ap, [(w3[:, j, :], act3[:, j, :]) for j in range(NJ)], [w, act16], [b])
                tt("dve", A3[:, dc, :], A3[:, dc, :], b.ap, ALU.add, [bufA, b], [bufA])
            ln_fm(bufA, bufA, h16, gb.ap[:, 16:24], gb.ap[:, 24:32], gb, tmp)
            dma(hresv[:, :, cs], A3, reads=[bufA])
            dma(hT16v[:, :, cs], h163, reads=[h16])
        phase_end("ffn%d" % l)

    def phase_out():
        phase_begin()
        hb = [A32.alloc(4096, name="hb%d" % i) for i in range(2)]
        ot = [A32.alloc(1024, name="ot%d" % i) for i in range(2)]
        for blk in range(NB):
            cs = slice(blk * 512, (blk + 1) * 512)
            hb_ = hb[blk % 2]; h3 = hb_.v3(8)
            dma(h3, hresv[:, :, cs], writes=[hb_])
            for sub in range(4):
                t = blk * 4 + sub
                o_ = ot[t % 2]
                for half in range(2):
                    b = bank()
                    for q in range(4):
                        kc = half * 4 + q
                        op("pe", lambda e, b=b, q=q, kc=kc, sub=sub, h3=h3: e.transpose(
                            out=b.ap[:, q * 128:(q + 1) * 128], in_=h3[:, kc, sub * 128:(sub + 1) * 128], identity=ident32),
                           [hb_, CST], [b])
                    tcopy("act" if half else "dve", o_.ap[:, half * 512:(half + 1) * 512], b.ap, [b], [o_])
                dma(out_d[t * 128:(t + 1) * 128, :], o_.ap, reads=[o_])
        phase_end("out")

    seq = [phase_rope, phase_ln_in, lambda: phase_weights(0), lambda: phase_weights(1)]
    for l in range(2):
        seq += [lambda l=l: phase_inproj(l), lambda l=l: phase_hgrn_sgu(l), lambda l=l: phase_attn(l), lambda l=l: phase_ffn(l)]
    seq += [phase_out]
    with st:
        skip = os.environ.get('KSKIP', '')
        for fi, f in enumerate(seq):
            if str(fi) in skip.split(','):
                continue
            f()
            if stopped[0]:
                break
        P.emit(nc)
    return nc


_CACHE = {}


def make_in_maps(inputs, n_cores=8):
    cst = host_consts()
    maps = []
    for c in range(n_cores):
        b = c % 4
        m = {"x": np.ascontiguousarray(inputs["x"][b]),
             "p": np.ascontiguousarray(inputs["p"][:, b]),
             "pos": np.ascontiguousarray(inputs["positions"][b:b + 1]).astype(np.int32),
             "cst": cst}
        for k in ("ln_in_g", "ln_in_b", "w_in", "hgrn_lb_logits", "hgrn_norm_g", "sgu_ln_g", "sgu_ln_b", "sgu_w_s",
                  "sgu_b_s", "mla_q_norm_g", "mla_w_uq", "mla_kv_norm_g", "mla_w_ukv", "w_out", "ln1_g", "ln1_b",
                  "w_gate_up", "w_down", "ple_w_gate", "ple_w_proj", "ln2_g", "ln2_b"):
            m[k] = np.ascontiguousarray(inputs[k])
        maps.append(m)
    return maps


def kernel(**inputs):
    inputs = {k: np.asarray(v) for k, v in inputs.items()}
    if "nc" not in _CACHE:
        _CACHE["nc"] = build()
    nc = _CACHE["nc"]
    maps = make_in_maps(inputs, 4)
    res = run_bass_kernel_spmd(nc, maps, core_ids=list(range(4)))
    out = np.stack([np.asarray(res.results[b]["out"]) for b in range(4)], axis=0)
    return out.astype(np.float32)
```

```python
import os
import contextlib
import numpy as np
import concourse.bass as bass
import concourse.mybir as mybir
from concourse.bass_utils import run_bass_kernel_spmd

F32 = mybir.dt.float32
BF16 = mybir.dt.bfloat16
I32 = mybir.dt.int32
AF = mybir.ActivationFunctionType
ALU = mybir.AluOpType
AX = mybir.AxisListType

S = 4096
D = 1024
NB = 8
NT = 32
DFF = 2816
NJ = 22
ALPHA = 4.0 ** 0.25
LN_EPS = 1e-5
RMS_EPS = 1e-6
ENGS = ["pe", "act", "dve", "pool", "sp"]
SAME_ENGINE_SYNC = os.environ.get('KSES', '1') == '1'
TWO_PI = 2.0 * np.pi
C1 = 6.28125
C2 = float(TWO_PI - 6.28125)


class Res:
    __slots__ = ("name", "w", "rs")

    def __init__(self, name=""):
        self.name = name
        self.w = None
        self.rs = []


class Op:
    __slots__ = ("eng", "fn", "deps", "is_dma", "signal", "sigval", "dsem", "dval", "dprev")

    def __init__(self, eng, fn, is_dma):
        self.eng = eng
        self.fn = fn
        self.deps = []
        self.is_dma = is_dma
        self.signal = False
        self.sigval = 0
        self.dsem = None
        self.dval = 0
        self.dprev = 0


class Prog:
    def __init__(self, n_dma_sems=8):
        self.ops = {e: [] for e in ENGS}
        self.n_dma_sems = n_dma_sems
        self.dma_count = {e: 0 for e in ENGS}
        self.last_dma = {}
        self.pending_bar = {e: None for e in ENGS}
        self.dummy = None

    def _add(self, op, reads, writes):
        deps = []
        raw = set()
        for r in reads:
            if r.w is not None:
                deps.append(r.w)
                raw.add(id(r.w))
        for w in writes:
            if w.w is not None:
                deps.append(w.w)
            deps.extend(w.rs)
        if self.pending_bar[op.eng] is not None:
            deps.extend(self.pending_bar[op.eng])
            self.pending_bar[op.eng] = None
        seen = set()
        for d in deps:
            if id(d) in seen or d is op:
                continue
            seen.add(id(d))
            if d.eng == op.eng and not d.is_dma and not op.is_dma and (op.eng == "pe" or not SAME_ENGINE_SYNC
                                                                        or id(d) not in raw):
                continue
            op.deps.append(d)
        for r in reads:
            r.rs.append(op)
        for w in writes:
            w.w = op
            w.rs = []
        self.ops[op.eng].append(op)
        return op

    def op(self, eng, fn, reads=(), writes=()):
        return self._add(Op(eng, fn, False), reads, writes)

    def dma(self, eng, fn, reads=(), writes=()):
        op = Op(eng, fn, True)
        op.dsem = (eng, self.dma_count[eng] % self.n_dma_sems)
        self.dma_count[eng] += 1
        self.last_dma[op.dsem] = op
        return self._add(op, reads, writes)

    def barrier(self):
        lasts = [self.ops[e][-1] for e in ENGS if self.ops[e] and not self.ops[e][-1].is_dma]
        lasts = []
        for e in ENGS:
            for o in reversed(self.ops[e]):
                if not o.is_dma:
                    lasts.append(o)
                    break
        lasts.extend(self.last_dma.values())
        for e in ENGS:
            prev = self.pending_bar[e] or []
            self.pending_bar[e] = list(prev) + lasts

    def emit(self, nc):
        for e in ENGS:
            for op in self.ops[e]:
                for d in op.deps:
                    if not d.is_dma:
                        d.signal = True
        for e in ENGS:
            n = 0
            for op in self.ops[e]:
                if op.signal and not op.is_dma:
                    n += 1
                    op.sigval = n
        cnt = {}
        for e in ENGS:
            for op in self.ops[e]:
                if op.is_dma:
                    c = cnt.get(op.dsem, 0)
                    op.dprev = c
                    op.dval = c + 16
                    cnt[op.dsem] = c + 16
        with contextlib.ExitStack() as st:
            esem = {e: st.enter_context(nc.semaphore("s_" + e)) for e in ENGS}
            dsem = {k: st.enter_context(nc.semaphore("d_%s_%d" % k)) for k in sorted(cnt.keys())}
            block = st.enter_context(nc.Block())

            def run_engine(ename, eng):
                waited = {}
                for op in self.ops[ename]:
                    if op.is_dma and op.dprev > 0:
                        key = ("d", op.dsem)
                        if waited.get(key, 0) < op.dprev:
                            eng.wait_ge(dsem[op.dsem], op.dprev)
                            waited[key] = op.dprev
                    for d in op.deps:
                        if d.is_dma:
                            key = ("d", d.dsem)
                            if waited.get(key, 0) < d.dval:
                                eng.wait_ge(dsem[d.dsem], d.dval)
                                waited[key] = d.dval
                        else:
                            key = ("e", d.eng)
                            if waited.get(key, 0) < d.sigval:
                                eng.wait_ge(esem[d.eng], d.sigval)
                                waited[key] = d.sigval
                    ins = op.fn(eng)
                    if op.is_dma:
                        ins.then_inc(dsem[op.dsem], 16)
                    elif op.signal:
                        if ename in ("dve", "pool") and self.dummy is not None:
                            ins = eng.memset(self.dummy[ename], 0.0)
                        ins.then_inc(esem[ename], 1)
                if ename == "sp":
                    for k, v in cnt.items():
                        eng.wait_ge(dsem[k], v)

            @block.tensor
            def _(eng):
                run_engine("pe", eng)

            @block.scalar
            def _(eng):
                run_engine("act", eng)

            @block.vector
            def _(eng):
                run_engine("dve", eng)

            @block.gpsimd
            def _(eng):
                run_engine("pool", eng)

            @block.sync
            def _(eng):
                run_engine("sp", eng)


class Tile:
    __slots__ = ("ap", "r")

    def __init__(self, ap, name=""):
        self.ap = ap
        self.r = Res(name)

    def v3(self, a):
        return self.ap.rearrange("p (a b) -> p a b", a=a)

    def v4(self, a, b):
        return self.ap.rearrange("p (a b c) -> p a b c", a=a, b=b)


class Arena:
    def __init__(self, t, n):
        self.t = t
        self.n = n
        self.off = 0
        self.base = 0

    def freeze(self):
        self.base = self.off

    def reset(self):
        self.off = self.base

    def alloc(self, cols, parts=128, name=""):
        assert self.off + cols <= self.n, ("arena overflow", name, self.off, cols, self.n)
        v = self.t[0:parts, self.off:self.off + cols]
        self.off += cols
        return Tile(v, name)


def host_consts():
    c = np.zeros((128, 520), np.float32)
    c[:, 0:128] = np.eye(128, dtype=np.float32)
    s = np.arange(128)[:, None]
    t = np.arange(128)[None, :]
    c[:, 128:256] = ((s // 32 == t // 32) & (s <= t)).astype(np.float32)
    c[:, 256:384] = (s // 32 == t // 32).astype(np.float32)
    c[:, 384:512] = (s <= t).astype(np.float32)
    c[:, 512:516] = (s // 32 == np.arange(4)[None, :]).astype(np.float32)
    inv_freq = (10000.0 ** (-np.arange(0, 32, 2, dtype=np.float32) / 32)).astype(np.float32)
    c[:, 516] = inv_freq[(np.arange(128) % 32) % 16]
    return c


def build(stop_after=None, debug=False):
    nc = bass.Bass("TRN2", target_bir_lowering=False)
    P = Prog(n_dma_sems=int(os.environ.get('KNS', 2)))
    dbgset = set(debug) if debug else set()

    def din(name, shape, dt=F32):
        return nc.dram_tensor(name, list(shape), dt, kind="ExternalInput").ap()

    def dscr(name, shape, dt):
        if name in dbgset:
            return nc.dram_tensor(name, list(shape), dt, kind="ExternalOutput").ap()
        return nc.dram_tensor(name, list(shape), dt).ap()

    x_d = din("x", [S, D])
    p_d = din("p", [2, S, 256])
    pos_d = din("pos", [1, S], I32)
    cst_d = din("cst", [128, 520])
    ln_in_g = din("ln_in_g", [D]); ln_in_b = din("ln_in_b", [D])
    w_in = din("w_in", [2, D, 2208])
    lb_logits = din("hgrn_lb_logits", [2, 256]); hg_ng = din("hgrn_norm_g", [2, 256])
    sgu_g = din("sgu_ln_g", [2, 256]); sgu_b = din("sgu_ln_b", [2, 256])
    sgu_w = din("sgu_w_s", [2, 4, 128, 128]); sgu_bs = din("sgu_b_s", [2, 4, 128])
    qng = din("mla_q_norm_g", [2, 384]); w_uq = din("mla_w_uq", [2, 384, 768])
    kvng = din("mla_kv_norm_g", [2, 256]); w_ukv = din("mla_w_ukv", [2, 256, 1024])
    w_out = din("w_out", [2, D, D])
    ln1_g = din("ln1_g", [2, D]); ln1_b = din("ln1_b", [2, D])
    w_gu = din("w_gate_up", [2, D, 2 * DFF]); w_dn = din("w_down", [2, DFF, D])
    w_pg = din("ple_w_gate", [2, D, D]); w_pp = din("ple_w_proj", [2, 256, D])
    ln2_g = din("ln2_g", [2, D]); ln2_b = din("ln2_b", [2, D])
    out_d = nc.dram_tensor("out", [S, D], F32, kind="ExternalOutput").ap()

    cosS = dscr("cosS", [128, S], F32); sinS = dscr("sinS", [128, S], F32)
    hT16 = dscr("hT16", [D, S], BF16); hres = dscr("hres", [D, S], F32)
    projTM = dscr("projTM", [S, 1536], F32)
    qnT = dscr("qnT", [512, S], BF16); qrT = dscr("qrT", [256, S], BF16)
    knT = dscr("knT", [512, S], BF16); krT = dscr("krT", [32, S], BF16)
    Vs = dscr("Vs", [S, 512], BF16)
    mixT = dscr("mixT", [D, S], BF16)
    Win16 = [dscr("Win16_%d" % l, [128, 8 * 2240], BF16) for l in range(2)]
    Wuq16 = [dscr("Wuq16_%d" % l, [128, 3 * 1024], BF16) for l in range(2)]
    Wukv16 = [dscr("Wukv16_%d" % l, [128, 2 * 1024], BF16) for l in range(2)]
    Wout16 = [dscr("Wout16_%d" % l, [128, 2 * 8 * 512], BF16) for l in range(2)]
    Wpg16 = [dscr("Wpg16_%d" % l, [128, 2 * 8 * 512], BF16) for l in range(2)]
    Wpp16 = [dscr("Wpp16_%d" % l, [128, 2 * 1024], BF16) for l in range(2)]
    Wgu16 = [dscr("Wgu16_%d" % l, [128, 11 * 2 * 8 * 256], BF16) for l in range(2)]
    Wd16 = [dscr("Wd16_%d" % l, [128, NJ * 1024], BF16) for l in range(2)]

    N16 = 69120
    N32 = 14848
    st = contextlib.ExitStack()
    st.enter_context(nc.allow_non_contiguous_dma("small strided parameter loads"))
    a16t = st.enter_context(nc.sbuf_tensor("A16", [128, N16], BF16))
    a32t = st.enter_context(nc.sbuf_tensor("A32", [128, N32], F32))
    i32t = st.enter_context(nc.sbuf_tensor("I32T", [128, 512], I32))
    A16 = Arena(a16t, N16)
    A32 = Arena(a32t, N32)
    I32T = Tile(i32t[:, :], "i32")
    PSB = [Tile(st.enter_context(nc.psum_tensor("ps%d" % i, [128, 512], F32))[:, :], "ps%d" % i) for i in range(7)]
    PS16 = Tile(st.enter_context(nc.psum_tensor("ps16", [128, 1024], BF16))[:, :], "ps16")
    bank_i = [0]

    def bank():
        b = PSB[bank_i[0] % 7]
        bank_i[0] += 1
        return b

    def dma(out, in_, reads=(), writes=(), q="sp"):
        P.dma(q, lambda e: e.dma_start(out=out, in_=in_), reads=[t.r for t in reads], writes=[t.r for t in writes])

    def op(eng, fn, reads=(), writes=()):
        P.op(eng, fn, reads=[t.r for t in reads], writes=[t.r for t in writes])

    def mm_group(out_ap, pairs, reads, writes, **kw):
        n = len(pairs)

        def fn(e):
            ins = None
            for i, (l, r) in enumerate(pairs):
                ins = e.matmul(out_ap, lhsT=l, rhs=r, start=(i == 0), stop=(i == n - 1), **kw)
            return ins
        op("pe", fn, reads, writes)

    def tcopy(eng, out_ap, in_ap, reads, writes):
        if eng == "act":
            op("act", lambda e: e.copy(out=out_ap, in_=in_ap), reads, writes)
        else:
            op(eng, lambda e: e.tensor_copy(out=out_ap, in_=in_ap), reads, writes)

    def tt(eng, out_ap, a, b, o, reads, writes):
        op(eng, lambda e: e.tensor_tensor(out=out_ap, in0=a, in1=b, op=o), reads, writes)

    def ts(eng, out_ap, a, s1, s2, o0, o1, reads, writes):
        if o1 is None:
            op(eng, lambda e: e.tensor_scalar(out=out_ap, in0=a, scalar1=s1, scalar2=None, op0=o0), reads, writes)
        else:
            op(eng, lambda e: e.tensor_scalar(out=out_ap, in0=a, scalar1=s1, scalar2=s2, op0=o0, op1=o1), reads, writes)

    def stt(out_ap, a, sc, b, o0, o1, reads, writes):
        op("dve", lambda e: e.scalar_tensor_tensor(out=out_ap, in0=a, scalar=sc, in1=b, op0=o0, op1=o1), reads, writes)

    def act(out_ap, in_ap, func, reads, writes, **kw):
        op("act", lambda e: e.activation(out=out_ap, in_=in_ap, func=func, **kw), reads, writes)

    CST = A32.alloc(520, name="cst")
    dma(CST.ap, cst_d, writes=[CST])
    ident32 = CST.ap[:, 0:128]
    hgmask = CST.ap[:, 128:256]
    lfull = CST.ap[:, 256:384]
    umask = CST.ap[:, 384:512]
    cmask = CST.ap[:, 512:516]
    invf = CST.ap[:, 516:517]
    C16 = A16.alloc(128 + 128 + 128, name="c16")
    ident16 = C16.ap[:, 0:128]
    umask16 = C16.ap[:, 128:256]
    ones16 = C16.ap[:, 256:384]
    tcopy("dve", ident16, ident32, [CST], [C16])
    tcopy("dve", umask16, umask, [CST], [C16])
    op("pool", lambda e: e.memset(ones16, 1.0), [], [C16])
    EPS = A32.alloc(2, name="eps")
    op("pool", lambda e: e.memset(EPS.ap[:, 0:1], LN_EPS), [], [EPS])
    op("pool", lambda e: e.memset(EPS.ap[:, 1:2], RMS_EPS), [], [EPS])
    eps_ln = EPS.ap[:, 0:1]
    eps_rms = EPS.ap[:, 1:2]
    DUM = A32.alloc(5 * 32, name="dummy")
    op("pool", lambda e: e.memset(DUM.ap, 0.0), [], [DUM])
    P.dummy = {"dve": DUM.ap[:, 0:32], "pool": DUM.ap[:, 32:64], "act": DUM.ap[:, 64:96], "act_src": DUM.ap[:, 128:160]}
    A16.freeze()
    A32.freeze()

    def dump(name, tile, cols, parts=128):
        if ("d_" + name) not in dbgset:
            return
        dt_ = nc.dram_tensor("d_" + name, [parts, cols], F32, kind="ExternalOutput").ap()
        dma(dt_, tile.ap[0:parts, 0:cols], reads=[tile])

    def phase_begin():
        P.barrier()
        A16.reset()
        A32.reset()

    stopped = [False]

    def phase_end(name):
        if stop_after == name:
            stopped[0] = True

    def ln_fm(y, o32, o16, gcol, bcol, gb_t, tmp):
        y3 = y.v3(8); o323 = o32.v3(8); o163 = o16.v3(8)
        y16, ysq16, mean, rstd, nmr, t1 = tmp
        y163 = y16.v3(8); ysq3 = ysq16.v3(8)
        tcopy("pool", y16.ap, y.ap, [y], [y16])
        act(ysq16.ap, y.ap, AF.Square, [y], [ysq16])
        b1 = bank(); b2 = bank()
        mm_group(b1.ap, [(ones16, y163[:, kc, :]) for kc in range(8)], [C16, y16], [b1])
        mm_group(b2.ap, [(ones16, ysq3[:, kc, :]) for kc in range(8)], [C16, ysq16], [b2])
        ts("dve", mean.ap, b1.ap, 1.0 / D, None, ALU.mult, None, [b1], [mean])
        tt("dve", t1.ap, mean.ap, mean.ap, ALU.mult, [mean], [t1])
        stt(t1.ap, b2.ap, 1.0 / D, t1.ap, ALU.mult, ALU.subtract, [b2, t1], [t1])
        act(t1.ap, t1.ap, AF.Ln, [t1, EPS], [t1], bias=eps_ln, scale=1.0)
        act(rstd.ap, t1.ap, AF.Exp, [t1], [rstd], scale=-0.5)
        stt(nmr.ap, mean.ap, -1.0, rstd.ap, ALU.mult, ALU.mult, [mean, rstd], [nmr])
        rb = rstd.ap.unsqueeze(1).to_broadcast([128, 8, 512])
        nb_ = nmr.ap.unsqueeze(1).to_broadcast([128, 8, 512])
        gbc = gcol.unsqueeze(2).to_broadcast([128, 8, 512])
        bbc = bcol.unsqueeze(2).to_broadcast([128, 8, 512])
        tt("dve", o323, y3, rb, ALU.mult, [y, rstd], [o32])
        tt("pool", o323, o323, nb_, ALU.add, [o32, nmr], [o32])
        tt("dve", o323, o323, gbc, ALU.mult, [o32, gb_t], [o32])
        tt("dve", o323, o323, bbc, ALU.add, [o32, gb_t], [o32])
        tcopy("act", o163, o323, [o32], [o16])

    def ln_tmp():
        return (A16.alloc(4096, name="y16"), A16.alloc(4096, name="ysq16"), A32.alloc(512, name="mean"),
                A32.alloc(512, name="rstd"), A32.alloc(512, name="nmr"), A32.alloc(512, name="t1"))

    def load_fm_cols(vecs, name):
        t = A32.alloc(8 * len(vecs), name=name)
        for i, v in enumerate(vecs):
            dma(t.ap[:, 8 * i:8 * i + 8], v.rearrange("(kc p) -> p kc", p=128), writes=[t])
        return t

    hT16v = hT16.rearrange("(kc p) t -> p kc t", p=128)
    hresv = hres.rearrange("(kc p) t -> p kc t", p=128)
    mixTv = mixT.rearrange("(kc p) t -> p kc t", p=128)

    def phase_rope():
        phase_begin()
        posf = A32.alloc(512, name="posf"); ang = A32.alloc(512, name="ang"); kf = A32.alloc(512, name="kf")
        r_ = A32.alloc(512, name="r"); res_ = [A32.alloc(512, name="res%d" % i) for i in range(2)]
        for c in range(8):
            cs = slice(c * 512, (c + 1) * 512)
            dma(I32T.ap, pos_d[:, cs].partition_broadcast(128), writes=[I32T])
            tcopy("dve", posf.ap, I32T.ap, [I32T], [posf])
            ts("dve", ang.ap, posf.ap, invf, None, ALU.mult, None, [posf, CST], [ang])
            for which, (shift, dst) in enumerate(((0.0, sinS), (float(np.pi / 2), cosS))):
                src = ang
                if shift != 0.0:
                    ts("dve", r_.ap, ang.ap, shift, None, ALU.add, None, [ang], [r_])
                    src = r_
                ts("dve", I32T.ap, src.ap, float(1.0 / TWO_PI), None, ALU.mult, None, [src], [I32T])
                tcopy("dve", kf.ap, I32T.ap, [I32T], [kf])
                stt(r_.ap, kf.ap, -C1, src.ap, ALU.mult, ALU.add, [kf, src], [r_])
                stt(r_.ap, kf.ap, -C2, r_.ap, ALU.mult, ALU.add, [kf, r_], [r_])
                ts("dve", r_.ap, r_.ap, 3.1415925, -3.1415925, ALU.min, ALU.max, [r_], [r_])
                rr = res_[which]
                act(rr.ap, r_.ap, AF.Sin, [r_], [rr])
                dma(dst[:, cs], rr.ap, reads=[rr])
        phase_end("rope")

    def phase_ln_in():
        phase_begin()
        gb = load_fm_cols([ln_in_g, ln_in_b], "gb_in")
        xt = [A32.alloc(1024, name="xt%d" % i) for i in range(2)]
        y = A32.alloc(4096, name="y"); o16 = A16.alloc(4096, name="o16")
        tmp = ln_tmp()
        for blk in range(NB):
            for sub in range(4):
                t = blk * 4 + sub
                xb = xt[t % 2]
                dma(xb.ap, x_d[t * 128:(t + 1) * 128, :], writes=[xb])
                for half in range(2):
                    b = bank()
                    for q in range(4):
                        kc = half * 4 + q
                        op("pe", lambda e, b=b, q=q, kc=kc, xb=xb: e.transpose(out=b.ap[:, q * 128:(q + 1) * 128],
                                                                            in_=xb.ap[:, kc * 128:(kc + 1) * 128],
                                                                            identity=ident32), [xb, CST], [b])
                    dst = y.v3(8)[:, half * 4:(half + 1) * 4, sub * 128:(sub + 1) * 128]
                    tcopy("act" if half else "dve", dst, b.ap.rearrange("p (q t) -> p q t", q=4), [b], [y])
            ln_fm(y, y, o16, gb.ap[:, 0:8], gb.ap[:, 8:16], gb, tmp)
            cs = slice(blk * 512, (blk + 1) * 512)
            dma(hresv[:, :, cs], y.v3(8), reads=[y])
            dma(hT16v[:, :, cs], o16.v3(8), reads=[o16])
        phase_end("ln_in")

    def phase_weights(l):
        phase_begin()
        stg = [A32.alloc(5632, name="wst%d" % i) for i in range(2)]
        o16 = [A16.alloc(5632, name="wo%d" % i) for i in range(2)]
        cnt = [0]
        gq = A32.alloc(3, name="gq"); gkv = A32.alloc(2, name="gkv")
        dma(gq.ap, qng[l].rearrange("(m p) -> p m", p=128), writes=[gq])
        dma(gkv.ap, kvng[l].rearrange("(m p) -> p m", p=128), writes=[gkv])
        engs = ["dve", "pool", "act"]

        def nxt():
            i = cnt[0]
            cnt[0] += 1
            return stg[i % 2], o16[i % 2], engs[i % 3]

        wsel = os.environ.get("KW_SEL", "in,uq,ukv,out,pp,gu,dn").split(",")
        Wv = Win16[l].rearrange("p (kc c) -> p kc c", kc=8)
        for kc in range(8 if "in" in wsel else 0):
            s_, o_, en = nxt()
            dma(s_.ap[:, 0:2208], w_in[l, kc * 128:(kc + 1) * 128, :], writes=[s_])
            tcopy(en, o_.ap[:, 0:2208], s_.ap[:, 0:2208], [s_], [o_])
            ts("dve", o_.ap[:, 2208:2224], s_.ap[:, 2192:2208], -1.0, None, ALU.mult, None, [s_], [o_])
            tcopy("dve", o_.ap[:, 2224:2240], s_.ap[:, 2176:2192], [s_], [o_])
            dma(Wv[:, kc, :], o_.ap[:, 0:2240], reads=[o_])
        Wv = Wuq16[l].rearrange("p (m c) -> p m c", m=3)
        for m in range(3 if 'uq' in wsel else 0):
            s_, o_, en = nxt()
            dma(s_.ap[:, 0:768], w_uq[l, m * 128:(m + 1) * 128, :], writes=[s_])
            ts("dve", s_.ap[:, 0:768], s_.ap[:, 0:768], gq.ap[:, m:m + 1], float(96.0 ** -0.5), ALU.mult, ALU.mult,
               [s_, gq], [s_])
            s3 = s_.ap[:, 0:768].rearrange("p (h e) -> p h e", e=96)
            tcopy("dve", o_.ap[:, 0:512].rearrange("p (h d) -> p h d", d=64), s3[:, :, 0:64], [s_], [o_])
            tcopy("pool", o_.ap[:, 512:768].rearrange("p (h r) -> p h r", r=32), s3[:, :, 64:96], [s_], [o_])
            rot = o_.ap[:, 768:1024].rearrange("p (h r) -> p h r", r=32)
            ts("dve", rot[:, :, 0:16], s3[:, :, 80:96], -1.0, None, ALU.mult, None, [s_], [o_])
            tcopy("pool", rot[:, :, 16:32], s3[:, :, 64:80], [s_], [o_])
            dma(Wv[:, m, :], o_.ap[:, 0:1024], reads=[o_])
        Wv = Wukv16[l].rearrange("p (m c) -> p m c", m=2)
        for m in range(2 if 'ukv' in wsel else 0):
            s_, o_, en = nxt()
            dma(s_.ap[:, 0:1024], w_ukv[l, m * 128:(m + 1) * 128, :], writes=[s_])
            ts("dve", s_.ap[:, 0:1024], s_.ap[:, 0:1024], gkv.ap[:, m:m + 1], None, ALU.mult, None, [s_, gkv], [s_])
            s3 = s_.ap[:, 0:1024].rearrange("p (h e) -> p h e", e=128)
            tcopy("dve", o_.ap[:, 0:512].rearrange("p (h d) -> p h d", d=64), s3[:, :, 0:64], [s_], [o_])
            tcopy("pool", o_.ap[:, 512:1024].rearrange("p (h d) -> p h d", d=64), s3[:, :, 64:128], [s_], [o_])
            dma(Wv[:, m, :], o_.ap[:, 0:1024], reads=[o_])
        for src, dst in ((w_out, Wout16), (w_pg, Wpg16)):
            Wv = dst[l].rearrange("p (dq kc c) -> p dq kc c", dq=2, kc=8)
            for kc in range(8 if 'out' in wsel else 0):
                s_, o_, en = nxt()
                dma(s_.ap[:, 0:1024], src[l, kc * 128:(kc + 1) * 128, :], writes=[s_])
                tcopy(en, o_.ap[:, 0:1024], s_.ap[:, 0:1024], [s_], [o_])
                dma(Wv[:, :, kc, :], o_.ap[:, 0:1024].rearrange("p (dq c) -> p dq c", dq=2), reads=[o_])
        Wv = Wpp16[l].rearrange("p (m c) -> p m c", m=2)
        for m in range(2 if 'pp' in wsel else 0):
            s_, o_, en = nxt()
            dma(s_.ap[:, 0:1024], w_pp[l, m * 128:(m + 1) * 128, :], writes=[s_])
            tcopy(en, o_.ap[:, 0:1024], s_.ap[:, 0:1024], [s_], [o_])
            dma(Wv[:, m, :], o_.ap[:, 0:1024], reads=[o_])
        Wv = Wgu16[l].rearrange("p (jg gu kc c) -> p gu jg kc c", jg=11, gu=2, kc=8)
        for kc in range(8 if 'gu' in wsel else 0):
            s_, o_, en = nxt()
            dma(s_.ap[:, 0:5632], w_gu[l, kc * 128:(kc + 1) * 128, :], writes=[s_])
            tcopy(en, o_.ap[:, 0:2816], s_.ap[:, 0:2816], [s_], [o_])
            tcopy(engs[(cnt[0] + 1) % 3], o_.ap[:, 2816:5632], s_.ap[:, 2816:5632], [s_], [o_])
            for gu in range(2):
                dma(Wv[:, gu, :, kc, :], o_.ap[:, gu * 2816:(gu + 1) * 2816].rearrange("p (jg c) -> p jg c", jg=11),
                    reads=[o_])
        Wv = Wd16[l].rearrange("p (j c) -> p j c", j=NJ)
        for j in range(int(os.environ.get('KDN', NJ)) if 'dn' in wsel else 0):
            s_, o_, en = nxt()
            dma(s_.ap[:, 0:1024], w_dn[l, j * 128:(j + 1) * 128, :], writes=[s_])
            tcopy(en, o_.ap[:, 0:1024], s_.ap[:, 0:1024], [s_], [o_])
            dma(Wv[:, j, :], o_.ap[:, 0:1024], reads=[o_])
        phase_end("weights%d" % l)

    def phase_inproj(l):
        phase_begin()
        Win = A16.alloc(8 * 2240, name="Win"); Wuq = A16.alloc(3 * 1024, name="Wuq"); Wukv = A16.alloc(2 * 1024, name="Wukv")
        dma(Win.ap, Win16[l], writes=[Win]); dma(Wuq.ap, Wuq16[l], writes=[Wuq]); dma(Wukv.ap, Wukv16[l], writes=[Wukv])
        Win3 = Win.v3(8); Wuq3 = Wuq.v3(3); Wukv3 = Wukv.v3(2)
        hTb = [A16.alloc(4096, name="hT%d" % i) for i in range(2)]
        csb = [A32.alloc(1024, name="cs%d" % i) for i in range(2)]
        cq32 = A32.alloc(1536, name="cq32"); rstd = A32.alloc(512, name="rstdq")
        sq16 = A16.alloc(1536, name="sq16"); cn16 = A16.alloc(1536, name="cn16")
        qn16 = A16.alloc(2048, name="qn16"); qr16 = A16.alloc(1024, name="qr16"); kn16 = A16.alloc(2048, name="kn16")
        kr16 = A16.alloc(512, name="kr16")
        v16 = [A16.alloc(512, name="v16_%d" % i) for i in range(2)]
        t1 = A32.alloc(512, name="t1"); t2 = A32.alloc(512, name="t2")
        ptm = [A32.alloc(1536, name="ptm%d" % i) for i in range(2)]
        cq323 = cq32.v3(3); sq3 = sq16.v3(3); cn3 = cn16.v3(3)
        for blk in range(int(os.environ.get('KNB', NB))):
            cs = slice(blk * 512, (blk + 1) * 512)
            hT = hTb[blk % 2]; hT3 = hT.v3(8)
            dma(hT3, hT16v[:, :, cs], writes=[hT])
            cb = csb[blk % 2]
            dma(cb.ap[:, 0:512], cosS[:, cs], writes=[cb]); dma(cb.ap[:, 512:1024], sinS[:, cs], writes=[cb])
            cosb = cb.ap[:, 0:512]; sinb = cb.ap[:, 512:1024]

            def lowrank(col0, nchunk, rank, W3, eps):
                for m in range(nchunk):
                    b = bank()
                    mm_group(b.ap, [(Win3[:, kc, col0 + m * 128:col0 + (m + 1) * 128], hT3[:, kc, :]) for kc in range(8)],
                             [Win, hT], [b])
                    tcopy("dve", cq323[:, m, :], b.ap, [b], [cq32])
                    act(sq3[:, m, :], cq323[:, m, :], AF.Square, [cq32], [sq16])
                lr = int(os.environ.get('KLR', 9))
                if lr < 2:
                    return
                b = bank()
                mm_group(b.ap, [(ones16, sq3[:, m, :]) for m in range(nchunk)], [C16, sq16], [b])
                if lr < 3:
                    return
                act(rstd.ap, b.ap, AF.Ln, [b, EPS], [rstd], bias=eps_rms, scale=1.0 / rank)
                act(rstd.ap, rstd.ap, AF.Exp, [rstd], [rstd], scale=-0.5)
                if lr < 4:
                    return
                for m in range(nchunk):
                    tt("dve" if m % 2 == 0 else "pool", cn3[:, m, :], cq323[:, m, :], rstd.ap, ALU.mult, [cq32, rstd], [cn16])

            def rope_fm(braw, brot, out_ap, parts, wr):
                tt("dve", t1.ap[0:parts, :], braw.ap[0:parts, :], cosb[0:parts, :], ALU.mult, [braw, cb], [t1])
                tt("dve", t2.ap[0:parts, :], brot.ap[0:parts, :], sinb[0:parts, :], ALU.mult, [brot, cb], [t2])
                tt("pool", out_ap, t1.ap[0:parts, :], t2.ap[0:parts, :], ALU.add, [t1, t2], [wr])

            parts = os.environ.get('KPARTS', 'q,kv,kr,tm').split(',')
            if 'q' in parts:
              lowrank(1536, 3, 384.0, Wuq3, RMS_EPS)
              for mc in range(4):
                  b = bank()
                  mm_group(b.ap, [(Wuq3[:, m, mc * 128:(mc + 1) * 128], cn3[:, m, :]) for m in range(3)], [Wuq, cn16], [b])
                  tcopy("act", qn16.v3(4)[:, mc, :], b.ap, [b], [qn16])
              for rc in range(2):
                  ba = bank(); bb = bank()
                  mm_group(ba.ap, [(Wuq3[:, m, 512 + rc * 128:512 + (rc + 1) * 128], cn3[:, m, :]) for m in range(3)],
                           [Wuq, cn16], [ba])
                  mm_group(bb.ap, [(Wuq3[:, m, 768 + rc * 128:768 + (rc + 1) * 128], cn3[:, m, :]) for m in range(3)],
                           [Wuq, cn16], [bb])
                  rope_fm(ba, bb, qr16.v3(2)[:, rc, :], 128, qr16)
              dma(qnT.rearrange("(mc p) t -> p mc t", p=128)[:, :, cs], qn16.v3(4), reads=[qn16])
              dma(qrT.rearrange("(rc p) t -> p rc t", p=128)[:, :, cs], qr16.v3(2), reads=[qr16])
            if 'lr' in parts:
              lowrank(1920, 2, 256.0, Wukv3, RMS_EPS)
            if 'kv' in parts:
              lowrank(1920, 2, 256.0, Wukv3, RMS_EPS)
              for mc in range(4):
                  b = bank()
                  mm_group(b.ap, [(Wukv3[:, m, mc * 128:(mc + 1) * 128], cn3[:, m, :]) for m in range(2)], [Wukv, cn16], [b])
                  tcopy("act", kn16.v3(4)[:, mc, :], b.ap, [b], [kn16])
              dma(knT.rearrange("(mc p) t -> p mc t", p=128)[:, :, cs], kn16.v3(4), reads=[kn16])
              for sub in range(4):
                  b = bank(); vv = v16[sub % 2]
                  mm_group(b.ap, [(cn3[:, m, sub * 128:(sub + 1) * 128], Wukv3[:, m, 512:1024]) for m in range(2)],
                           [Wukv, cn16], [b])
                  tcopy("dve", vv.ap, b.ap, [b], [vv])
                  t = blk * 4 + sub
                  dma(Vs[t * 128:(t + 1) * 128, :], vv.ap, reads=[vv])
            if 'kr' in parts:
              ba = bank(); bb = bank()
              mm_group(ba.ap[0:32, :], [(Win3[:, kc, 2176:2208], hT3[:, kc, :]) for kc in range(8)], [Win, hT], [ba])
              mm_group(bb.ap[0:32, :], [(Win3[:, kc, 2208:2240], hT3[:, kc, :]) for kc in range(8)], [Win, hT], [bb])
              rope_fm(ba, bb, kr16.ap[0:32, :], 32, kr16)
              dma(krT[:, cs], kr16.ap[0:32, :], reads=[kr16])
            for sub in range(4 if 'tm' in parts else 0):
                t = blk * 4 + sub
                pt = ptm[t % 2]
                for cbk in range(3):
                    b = bank()
                    mm_group(b.ap, [(hT3[:, kc, sub * 128:(sub + 1) * 128], Win3[:, kc, cbk * 512:(cbk + 1) * 512])
                                    for kc in range(8)], [Win, hT], [b])
                    tcopy("act" if cbk == 1 else "dve", pt.ap[:, cbk * 512:(cbk + 1) * 512], b.ap, [b], [pt])
                dma(projTM[t * 128:(t + 1) * 128, :], pt.ap, reads=[pt])
        phase_end("inproj%d" % l)

    def phase_hgrn_sgu(l):
        phase_begin()
        par = A32.alloc(256 * 6, name="par")
        lbt = par.ap[:, 0:256]; omlb = par.ap[:, 256:512]; ngt = par.ap[:, 512:768]
        sgt = par.ap[:, 768:1024]; sbt = par.ap[:, 1024:1280]; ltmp = par.ap[:, 1280:1536]
        dma(ngt, hg_ng[l:l + 1, :].partition_broadcast(128), writes=[par])
        dma(sgt, sgu_g[l:l + 1, :].partition_broadcast(128), writes=[par])
        dma(sbt, sgu_b[l:l + 1, :].partition_broadcast(128), writes=[par])
        if l == 0:
            op("dve", lambda e: e.memset(lbt, 0.0), [], [par])
            op("dve", lambda e: e.memset(omlb, 1.0), [], [par])
        else:
            dma(lbt, lb_logits[0:1, :].partition_broadcast(128), writes=[par])
            dma(ltmp, lb_logits[1:2, :].partition_broadcast(128), writes=[par])
            tt("dve", lbt, lbt, ltmp, ALU.subtract, [par], [par])
            act(lbt, lbt, AF.Exp, [par], [par])
            ts("dve", lbt, lbt, 1.0, None, ALU.add, None, [par], [par])
            op("dve", lambda e: e.reciprocal(out=lbt, in_=lbt), [par], [par])
            ts("dve", omlb, lbt, -1.0, 1.0, ALU.mult, ALU.add, [par], [par])
        bscol = A32.alloc(4, name="bscol")
        dma(bscol.ap, sgu_bs[l].rearrange("g t -> t g"), writes=[bscol])
        WsT = A32.alloc(512, name="WsT"); wtmp = A32.alloc(512, name="wtmp")
        dma(wtmp.v3(4), sgu_w[l].rearrange("g t s -> t g s"), writes=[wtmp])
        b = bank()
        for g in range(4):
            op("pe", lambda e, g=g, b=b: e.transpose(out=b.ap[:, g * 128:(g + 1) * 128], in_=wtmp.ap[:, g * 128:(g + 1) * 128],
                                                  identity=ident32), [wtmp, CST], [b])
        tt("dve", WsT.v3(4), b.ap.rearrange("p (g t) -> p g t", g=4), umask.unsqueeze(1).to_broadcast([128, 4, 128]),
           ALU.mult, [b, CST], [WsT])
        Sh = A32.alloc(5 * 128, name="Sh")
        Sh4 = Sh.v4(5, 2)
        op("pool", lambda e: e.memset(Sh.ap, 0.0), [], [Sh])
        X = [A32.alloc(1536, name="X%d" % i) for i in range(2)]
        GU = A32.alloc(512, name="GU"); E = A32.alloc(768, name="E")
        f_ = A32.alloc(256, name="f"); logf = A32.alloc(256, name="logf"); omf = A32.alloc(256, name="omf")
        qs = A32.alloc(256, name="qs"); gs = A32.alloc(256, name="gs")
        eb = A32.alloc(768, name="eb")
        QK = A32.alloc(512, name="QK")
        Ke = A32.alloc(256, name="Ke"); Vexp = A32.alloc(1024, name="Vexp")
        QKT = A32.alloc(512, name="QKT")
        PTm = A32.alloc(512, name="PTm"); dcol = A32.alloc(8, name="dcol"); T2 = A32.alloc(640, name="T2")
        oi = A32.alloc(256, name="oi"); osq = A32.alloc(256, name="osq"); st4 = A32.alloc(4, name="st4")
        vn = A32.alloc(256, name="vn"); bst = A32.alloc(8, name="bst"); mv = A32.alloc(2, name="mv"); rs1 = A32.alloc(1, name="rs1")
        O16 = [A16.alloc(512, name="O16_%d" % i) for i in range(2)]
        MT = [A16.alloc(2048, name="MT%d" % i) for i in range(2)]
        for t in range(int(os.environ.get('KNT', NT))):
            Xt = X[t % 2]
            dma(Xt.ap, projTM[t * 128:(t + 1) * 128, :], writes=[Xt])
            xq = Xt.ap[:, 0:256]; xf = Xt.ap[:, 256:512]; xi = Xt.ap[:, 512:768]; xg = Xt.ap[:, 768:1024]
            O = O16[t % 2]
            act(GU.ap, Xt.ap[:, 1024:1536], AF.Gelu, [Xt], [GU])
            op("dve", lambda e: e.tensor_reduce(out=bst.ap[:, 0:1], in_=GU.ap[:, 256:512].unsqueeze(1), axis=AX.X, op=ALU.add),
               [GU], [bst])
            tt("dve", vn.ap, GU.ap[:, 256:512], GU.ap[:, 256:512], ALU.mult, [GU], [vn])
            op("dve", lambda e: e.tensor_reduce(out=bst.ap[:, 1:2], in_=vn.ap.unsqueeze(1), axis=AX.X, op=ALU.add), [vn], [bst])
            ts("dve", mv.ap[:, 0:1], bst.ap[:, 0:1], 1.0 / 256, None, ALU.mult, None, [bst], [mv])
            tt("dve", bst.ap[:, 2:3], mv.ap[:, 0:1], mv.ap[:, 0:1], ALU.mult, [mv], [bst])
            stt(mv.ap[:, 1:2], bst.ap[:, 1:2], 1.0 / 256, bst.ap[:, 2:3], ALU.mult, ALU.subtract, [bst], [mv])
            act(rs1.ap, mv.ap[:, 1:2], AF.Ln, [mv, EPS], [rs1], bias=eps_ln, scale=1.0)
            act(rs1.ap, rs1.ap, AF.Exp, [rs1], [rs1], scale=-0.5)
            ts("dve", vn.ap, GU.ap[:, 256:512], mv.ap[:, 0:1], rs1.ap[:, 0:1], ALU.subtract, ALU.mult, [GU, mv, rs1], [vn])
            tt("pool", vn.ap, vn.ap, sgt, ALU.mult, [vn, par], [vn])
            tt("pool", vn.ap, vn.ap, sbt, ALU.add, [vn, par], [vn])
            bz = bank()

            def zfn(e, bz=bz):
                ins = None
                for g in range(4):
                    ins = e.matmul(bz.ap[:, g * 64:(g + 1) * 64], lhsT=WsT.ap[:, g * 128:(g + 1) * 128],
                                   rhs=vn.ap[:, g * 64:(g + 1) * 64], start=True, stop=True)
                return ins
            op("pe", zfn, [WsT, vn], [bz])
            for g in range(4):
                stt(O.ap[:, 256 + g * 64:256 + (g + 1) * 64], bz.ap[:, g * 64:(g + 1) * 64], bscol.ap[:, g:g + 1],
                    GU.ap[:, g * 64:(g + 1) * 64], ALU.add, ALU.mult, [bz, bscol, GU], [O])
            act(E.ap[:, 0:512], Xt.ap[:, 0:512], AF.Exp, [Xt], [E], scale=-1.0)
            act(E.ap[:, 512:768], xg, AF.Exp, [Xt], [E], scale=-1.0)
            ts("dve", E.ap, E.ap, 1.0, None, ALU.add, None, [E], [E])
            op("dve", lambda e: e.reciprocal(out=E.ap, in_=E.ap), [E], [E])
            tt("pool", qs.ap, xq, E.ap[:, 0:256], ALU.mult, [Xt, E], [qs])
            tt("pool", gs.ap, xg, E.ap[:, 512:768], ALU.mult, [Xt, E], [gs])
            tt("dve", f_.ap, E.ap[:, 256:512], omlb, ALU.mult, [E, par], [f_])
            tt("dve", f_.ap, f_.ap, lbt, ALU.add, [f_, par], [f_])
            act(logf.ap, f_.ap, AF.Ln, [f_], [logf])
            ts("pool", omf.ap, f_.ap, -1.0, 1.0, ALU.mult, ALU.add, [f_], [omf])
            bb_ = bank()
            op("pe", lambda e, bb_=bb_: e.matmul(bb_.ap[:, 0:256], lhsT=hgmask, rhs=logf.ap, start=True, stop=True),
               [CST, logf], [bb_])
            op("pe", lambda e, bb_=bb_: e.matmul(bb_.ap[:, 256:512], lhsT=lfull, rhs=logf.ap, start=True, stop=True),
               [CST, logf], [bb_])
            bd = bank()

            def dfn(e, bd=bd):
                ins = None
                for j in range(2):
                    ins = e.matmul(bd.ap[:, j * 4:(j + 1) * 4], lhsT=logf.ap[:, j * 128:(j + 1) * 128], rhs=cmask,
                                   start=True, stop=True)
                return ins
            op("pe", dfn, [logf, CST], [bd])
            act(eb.ap[:, 0:256], bb_.ap[:, 0:256], AF.Exp, [bb_], [eb])
            act(eb.ap[:, 256:512], bb_.ap[:, 0:256], AF.Exp, [bb_], [eb], scale=-1.0)
            act(eb.ap[:, 512:768], bb_.ap[:, 256:512], AF.Exp, [bb_], [eb])
            act(dcol.ap, bd.ap[:, 0:8], AF.Exp, [bd], [dcol])
            tt("dve", QK.ap[:, 0:256], qs.ap, eb.ap[:, 0:256], ALU.mult, [qs, eb], [QK])
            tt("dve", QK.ap[:, 256:512], omf.ap, eb.ap[:, 256:512], ALU.mult, [omf, eb], [QK])
            tt("dve", Ke.ap, QK.ap[:, 256:512], eb.ap[:, 512:768], ALU.mult, [QK, eb], [Ke])
            tt("pool", Vexp.ap.rearrange("p (h c v) -> p h c v", h=4, c=4),
               xi.rearrange("p (h v) -> p h v", h=4).unsqueeze(2).to_broadcast([128, 4, 4, 64]),
               cmask.unsqueeze(1).unsqueeze(3).to_broadcast([128, 4, 4, 64]), ALU.mult, [Xt, CST], [Vexp])
            btr = bank()
            for i4 in range(4):
                op("pe", lambda e, i4=i4, btr=btr: e.transpose(out=btr.ap[:, i4 * 128:(i4 + 1) * 128],
                                                              in_=QK.ap[:, i4 * 128:(i4 + 1) * 128], identity=ident32),
                   [QK, CST], [btr])
            tcopy("act", QKT.ap, btr.ap, [btr], [QKT])
            bsc = [bank(), bank()]

            def scfn(e, bsc=bsc):
                ins = None
                for h in range(4):
                    j, ee = h // 2, h % 2
                    ins = e.matmul(bsc[ee].ap[:, j * 128:(j + 1) * 128],
                                   lhsT=QKT.ap[ee * 64:(ee + 1) * 64, 256 + j * 128:256 + (j + 1) * 128],
                                   rhs=QKT.ap[ee * 64:(ee + 1) * 64, j * 128:(j + 1) * 128], start=True, stop=True)
                return ins
            op("pe", scfn, [QKT], bsc)
            for h in range(4):
                j, ee = h // 2, h % 2
                tt("dve", PTm.ap[:, h * 128:(h + 1) * 128], bsc[ee].ap[:, j * 128:(j + 1) * 128], hgmask, ALU.mult,
                   [bsc[ee], CST], [PTm])
            bkv = [bank(), bank()]

            def kvfn(e, bkv=bkv):
                ins = None
                for h in range(4):
                    j = h // 2
                    ins = e.matmul(bkv[h // 2].ap[:, (h % 2) * 256:(h % 2 + 1) * 256], lhsT=Ke.ap[:, j * 128:(j + 1) * 128],
                                   rhs=Vexp.ap[:, h * 256:(h + 1) * 256], start=True, stop=True)
                return ins
            op("pe", kvfn, [Ke, Vexp], bkv)
            for c in range(4):
                for h in range(4):
                    j, ee = h // 2, h % 2
                    rows = slice(ee * 64, (ee + 1) * 64)
                    stt(Sh4[rows, c + 1, j, :], Sh4[rows, c, j, :], dcol.ap[rows, j * 4 + c:j * 4 + c + 1],
                        bkv[j].ap[rows, ee * 256 + c * 64:ee * 256 + (c + 1) * 64], ALU.mult, ALU.add,
                        [Sh, dcol, bkv[j]], [Sh])
            bo = [bank(), bank()]
            boi = bank()

            def ofn(e, bo=bo, boi=boi, xi=xi):
                ins = None
                for h in range(4):
                    ins = e.matmul(boi.ap[:, h * 64:(h + 1) * 64], lhsT=PTm.ap[:, h * 128:(h + 1) * 128],
                                   rhs=xi[:, h * 64:(h + 1) * 64], start=True, stop=True)
                for h in range(4):
                    j, ee = h // 2, h % 2
                    for c in range(4):
                        ins = e.matmul(bo[ee].ap[32 * c:32 * c + 32, j * 64:(j + 1) * 64],
                                       lhsT=QKT.ap[ee * 64:(ee + 1) * 64, j * 128 + 32 * c:j * 128 + 32 * c + 32],
                                       rhs=Sh4[ee * 64:(ee + 1) * 64, c, j, :], start=True, stop=True,
                                       tile_position=(ee * 64, 32 * c))
                return ins
            op("pe", ofn, [PTm, Xt, QKT, Sh], bo + [boi])
            tcopy("act", T2.ap[:, 0:128], bo[0].ap[:, 0:128], [bo[0]], [T2])
            tcopy("act", T2.ap[:, 256:512], boi.ap[:, 0:256], [boi], [T2])
            tcopy("act", T2.ap[:, 512:640], bo[1].ap[:, 0:128], [bo[1]], [T2])
            for h in range(4):
                j, ee = h // 2, h % 2
                src = T2.ap[:, j * 64:(j + 1) * 64] if ee == 0 else T2.ap[:, 512 + j * 64:512 + (j + 1) * 64]
                tt("dve", oi.ap[:, h * 64:(h + 1) * 64], T2.ap[:, 256 + h * 64:256 + (h + 1) * 64], src, ALU.add, [T2], [oi])
            tcopy("pool", Sh4[:, 0, :, :], Sh4[:, 4, :, :], [Sh], [Sh])
            tt("dve", osq.ap, oi.ap, oi.ap, ALU.mult, [oi], [osq])
            op("dve", lambda e: e.tensor_reduce(out=st4.ap, in_=osq.v3(4), axis=AX.X, op=ALU.add), [osq], [st4])
            act(st4.ap, st4.ap, AF.Ln, [st4, EPS], [st4], bias=eps_rms, scale=1.0 / 64)
            act(st4.ap, st4.ap, AF.Exp, [st4], [st4], scale=-0.5)
            tt("dve", oi.v3(4), oi.v3(4), st4.ap.unsqueeze(2).to_broadcast([128, 4, 64]), ALU.mult, [oi, st4], [oi])
            tt("pool", oi.ap, oi.ap, ngt, ALU.mult, [oi, par], [oi])
            tt("dve", O.ap[:, 0:256], oi.ap, gs.ap, ALU.mult, [oi, gs], [O])
            if t == 0:
                for nm, tl, cc in (("X", Xt, 1536), ("GU", GU, 512), ("vn", vn, 256), ("E", E, 768), ("logf", logf, 256),
                                   ("eb", eb, 768), ("QK", QK, 512), ("Ke", Ke, 256), ("PTm", PTm, 512), ("Sh", Sh, 640),
                                   ("oi", oi, 256), ("T2", T2, 640), ("dcol", dcol, 8), ("QKT", QKT, 512), ("mv", mv, 2)):
                    dump(nm, tl, cc)
            for i4 in range(4):
                op("pe", lambda e, i4=i4, O=O: e.transpose(out=PS16.ap[:, i4 * 128:(i4 + 1) * 128],
                                                          in_=O.ap[:, i4 * 128:(i4 + 1) * 128], identity=ident16),
                   [O, C16], [PS16])
            mt = MT[(t // 4) % 2]
            tcopy("act", mt.v3(4)[:, :, (t % 4) * 128:(t % 4 + 1) * 128], PS16.ap[:, 0:512].rearrange("p (c t) -> p c t", c=4),
                  [PS16], [mt])
            if t % 4 == 3:
                blk = t // 4
                dma(mixTv[:, 0:4, blk * 512:(blk + 1) * 512], mt.v3(4), reads=[mt])
        phase_end("hgrn%d" % l)

    def phase_attn(l):
        phase_begin()
        KT = A16.alloc(8 * S, parts=96, name="KT")
        KT3 = KT.v3(8)
        dma(KT3[0:64, :, :], knT.rearrange("(h d) t -> d h t", d=64), writes=[KT])
        for h in range(8):
            dma(KT3[64:96, h, :], krT, writes=[KT])
        Va = A16.alloc(32 * 8 * 65, name="Va")
        Va4 = Va.ap.rearrange("p (n h v) -> p n h v", n=32, h=8)
        op("pool", lambda e: e.memset(Va.ap, 1.0), [], [Va])
        vst = [A16.alloc(1024, name="vst%d" % i) for i in range(2)]
        for q in range(16):
            vs_ = vst[q % 2]
            dma(vs_.v3(2), Vs.rearrange("(n p) c -> p n c", p=128)[:, q * 2:(q + 1) * 2, :], writes=[vs_])
            tcopy("dve" if q % 2 == 0 else "pool", Va4[:, q * 2:(q + 1) * 2, :, 0:64],
                  vs_.ap.rearrange("p (n h v) -> p n h v", n=2, h=8), [vs_], [Va])
        Qb = [A16.alloc(8 * 512, parts=96, name="Qb%d" % i) for i in range(2)]
        PT = [A16.alloc(512, name="PT%d" % i) for i in range(4)]
        oc = [A16.alloc(512, name="oc%d" % i) for i in range(4)]
        mc = [A16.alloc(2048, name="mc%d" % i) for i in range(1)] * 2
        rinv = A32.alloc(8, name="rinv")
        accs = A32.alloc(260, name="accs")
        PVB = PSB[4:6]
        pvi = [0]
        SC = PSB[0:4]
        ACC = PSB[4:6]
        pti = 0
        sci = 0
        for blk in range(NB):
            cs = slice(blk * 512, (blk + 1) * 512)
            Q = Qb[blk % 2]; Q3 = Q.v3(8)
            dma(Q3[0:64, :, :], qnT.rearrange("(h d) t -> d h t", d=64)[:, :, cs], writes=[Q])
            dma(Q3[64:96, :, :], qrT.rearrange("(h r) t -> r h t", r=32)[:, :, cs], writes=[Q])
            for h in range(8):
                acc = ACC[h % 2]
                nkt = 4 * blk + 4
                for kt in range(nkt):
                    q0 = 0 if kt <= 4 * blk else (kt - 4 * blk) * 128
                    n = 512 - q0
                    sc = SC[sci % 4]; sci += 1
                    pt = PT[pti % 4]; pti += 1
                    op("pe", lambda e, sc=sc, h=h, kt=kt, q0=q0, n=n, Q3=Q3: e.matmul(
                        sc.ap[:, 0:n], lhsT=KT3[0:96, h, kt * 128:(kt + 1) * 128], rhs=Q3[0:96, h, q0:512],
                        start=True, stop=True), [KT, Q], [sc])
                    act(pt.ap[:, 0:n], sc.ap[:, 0:n], AF.Exp, [sc], [pt])
                    if kt >= 4 * blk:
                        tt("pool", pt.ap[:, 0:128], pt.ap[:, 0:128], umask16, ALU.mult, [pt, C16], [pt])

                    pvb = PVB[pvi[0] % 2]; pvi[0] += 1
                    qs0 = q0 // 128

                    def pvfn(e, pt=pt, pvb=pvb, h=h, kt=kt, q0=q0):
                        ins = None
                        for qsub in range(q0 // 128, 4):
                            ins = e.matmul(pvb.ap[:, qsub * 65:(qsub + 1) * 65],
                                           lhsT=pt.ap[:, qsub * 128 - q0:qsub * 128 - q0 + 128], rhs=Va4[:, kt, h, :],
                                           start=True, stop=True)
                        return ins
                    op("pe", pvfn, [pt, Va], [pvb])
                    if kt == 0:
                        tcopy("dve", accs.ap[:, 0:260], pvb.ap[:, 0:260], [pvb], [accs])
                    else:
                        tt("dve", accs.ap[:, qs0 * 65:260], accs.ap[:, qs0 * 65:260], pvb.ap[:, qs0 * 65:260], ALU.add,
                           [accs, pvb], [accs])
                a3 = accs.ap[:, 0:260].rearrange("p (q v) -> p q v", q=4)
                op("dve", lambda e, a3=a3: e.reciprocal(out=rinv.ap[:, 0:4], in_=a3[:, :, 64]), [accs], [rinv])
                for qsub in range(4):
                    ts("dve", oc[qsub].ap[:, h * 64:(h + 1) * 64], accs.ap[:, qsub * 65:qsub * 65 + 64],
                       rinv.ap[:, qsub:qsub + 1], None, ALU.mult, None, [accs, rinv], [oc[qsub]])
            for qsub in range(4):
                t = blk * 4 + qsub
                for i4 in range(4):
                    op("pe", lambda e, i4=i4, qsub=qsub: e.transpose(out=PS16.ap[:, i4 * 128:(i4 + 1) * 128],
                                                                    in_=oc[qsub].ap[:, i4 * 128:(i4 + 1) * 128],
                                                                    identity=ident16), [oc[qsub], C16], [PS16])
                m_ = mc[blk % 2]
                tcopy("act", m_.v3(4)[:, :, qsub * 128:(qsub + 1) * 128], PS16.ap[:, 0:512].rearrange("p (c t) -> p c t", c=4),
                      [PS16], [m_])
            dma(mixTv[:, 4:8, cs], mc[blk % 2].v3(4), reads=[mc[blk % 2]])
        phase_end("attn%d" % l)

    def phase_ffn(l):
        phase_begin()
        gb = load_fm_cols([ln1_g[l], ln1_b[l], ln2_g[l], ln2_b[l]], "gb")
        WR = [A16.alloc(4096, name="wr%d" % i) for i in range(3)]
        wri = [0]

        def wslot():
            w = WR[wri[0] % 3]
            wri[0] += 1
            return w
        mixb = A16.alloc(4096, name="mixb"); pT16 = A16.alloc(1024, name="pT16")
        h16 = A16.alloc(4096, name="h16"); act16 = A16.alloc(NJ * 512, name="act16")
        bufA = A32.alloc(4096, name="bufA"); bufB = A32.alloc(4096, name="bufB")
        ptile = [A32.alloc(256, name="pt%d" % i) for i in range(2)]
        sg = [A32.alloc(512, name="sg%d" % i) for i in range(2)]
        tmp = ln_tmp()
        A3 = bufA.v3(8); B3 = bufB.v3(8); h163 = h16.v3(8); mix3 = mixb.v3(8); act3 = act16.v3(NJ); pT3 = pT16.v3(2)
        Woutv = Wout16[l].rearrange("p (dq x) -> p dq x", dq=2)
        Wpgv = Wpg16[l].rearrange("p (dq x) -> p dq x", dq=2)
        Wguv = Wgu16[l].rearrange("p (jg x) -> p jg x", jg=11)
        Wdv = Wd16[l].rearrange("p (j c) -> p j c", j=NJ)
        for blk in range(NB):
            cs = slice(blk * 512, (blk + 1) * 512)
            dma(mix3, mixTv[:, :, cs], writes=[mixb])
            dma(A3, hresv[:, :, cs], writes=[bufA])
            for sub in range(4):
                t = blk * 4 + sub
                pt_ = ptile[t % 2]
                dma(pt_.ap, p_d[l, t * 128:(t + 1) * 128, :], writes=[pt_])
                b = bank()
                for m in range(2):
                    op("pe", lambda e, b=b, m=m, pt_=pt_: e.transpose(out=b.ap[:, m * 128:(m + 1) * 128],
                                                                     in_=pt_.ap[:, m * 128:(m + 1) * 128], identity=ident32),
                       [pt_, CST], [b])
                tcopy("act", pT3[:, :, sub * 128:(sub + 1) * 128], b.ap[:, 0:256].rearrange("p (m t) -> p m t", m=2), [b], [pT16])
            for dq in range(2):
                w = wslot(); w3 = w.v3(8)
                dma(w.ap, Woutv[:, dq, :], writes=[w])
                for d4 in range(4):
                    dc = dq * 4 + d4
                    b = bank()
                    mm_group(b.ap, [(w3[:, kc, d4 * 128:(d4 + 1) * 128], mix3[:, kc, :]) for kc in range(8)], [w, mixb], [b])
                    stt(B3[:, dc, :], A3[:, dc, :], ALPHA, b.ap, ALU.mult, ALU.add, [bufA, b], [bufB])
            ln_fm(bufB, bufB, h16, gb.ap[:, 0:8], gb.ap[:, 8:16], gb, tmp)
            wp = wslot(); wp3 = wp.ap[:, 0:2048].rearrange("p (m c) -> p m c", m=2)
            dma(wp.ap[:, 0:2048], Wpp16[l], writes=[wp])
            for dq in range(2):
                w = wslot(); w3 = w.v3(8)
                dma(w.ap, Wpgv[:, dq, :], writes=[w])
                for d4 in range(4):
                    dc = dq * 4 + d4
                    b = bank(); b2 = bank(); s_ = sg[dc % 2]
                    mm_group(b.ap, [(w3[:, kc, d4 * 128:(d4 + 1) * 128], h163[:, kc, :]) for kc in range(8)], [w, h16], [b])
                    mm_group(b2.ap, [(wp3[:, m, dc * 128:(dc + 1) * 128], pT3[:, m, :]) for m in range(2)], [wp, pT16], [b2])
                    act(s_.ap, b.ap, AF.Sigmoid, [b], [s_])
                    tt("dve", A3[:, dc, :], s_.ap, b2.ap, ALU.mult, [s_, b2], [bufA])
                    stt(A3[:, dc, :], B3[:, dc, :], ALPHA, A3[:, dc, :], ALU.mult, ALU.add, [bufB, bufA], [bufA])
            for jg in range(11):
                w = wslot(); w4 = w.ap.rearrange("p (gu kc c) -> p gu kc c", gu=2, kc=8)
                dma(w.ap, Wguv[:, jg, :], writes=[w])
                for jj in range(2):
                    j = jg * 2 + jj
                    bg = bank(); bu = bank(); s_ = sg[j % 2]
                    mm_group(bg.ap, [(w4[:, 0, kc, jj * 128:(jj + 1) * 128], h163[:, kc, :]) for kc in range(8)], [w, h16], [bg])
                    mm_group(bu.ap, [(w4[:, 1, kc, jj * 128:(jj + 1) * 128], h163[:, kc, :]) for kc in range(8)], [w, h16], [bu])
                    act(s_.ap, bg.ap, AF.Silu, [bg], [s_])
                    tt("dve", act3[:, j, :], s_.ap, bu.ap, ALU.mult, [s_, bu], [act16])
            for dc in range(8):
                w = wslot(); w3 = w.ap[:, 0:NJ * 128].rearrange("p (j c) -> p j c", j=NJ)
                dma(w3, Wdv[:, :, dc * 128:(dc + 1) * 128], writes=[w])
                b = bank()
                mm_group(b.ap, [(w3[:, j, :], act3[:, j, :]) for j in range(NJ)], [w, act16], [b])
                tt("dve", A3[:, dc, :], A3[:, dc, :], b.ap, ALU.add, [bufA, b], [bufA])
            ln_fm(bufA, bufA, h16, gb.ap[:, 16:24], gb.ap[:, 24:32], gb, tmp)
            dma(hresv[:, :, cs], A3, reads=[bufA])
            dma(hT16v[:, :, cs], h163, reads=[h16])
        phase_end("ffn%d" % l)

    def phase_out():
        phase_begin()
        hb = [A32.alloc(4096, name="hb%d" % i) for i in range(2)]
        ot = [A32.alloc(1024, name="ot%d" % i) for i in range(2)]
        for blk in range(NB):
            cs = slice(blk * 512, (blk + 1) * 512)
            hb_ = hb[blk % 2]; h3 = hb_.v3(8)
            dma(h3, hresv[:, :, cs], writes=[hb_])
            for sub in range(4):
                t = blk * 4 + sub
                o_ = ot[t % 2]
                for half in range(2):
                    b = bank()
                    for q in range(4):
                        kc = half * 4 + q
                        op("pe", lambda e, b=b, q=q, kc=kc, sub=sub, h3=h3: e.transpose(
                            out=b.ap[:, q * 128:(q + 1) * 128], in_=h3[:, kc, sub * 128:(sub + 1) * 128], identity=ident32),
                           [hb_, CST], [b])
                    tcopy("act" if half else "dve", o_.ap[:, half * 512:(half + 1) * 512], b.ap, [b], [o_])
                dma(out_d[t * 128:(t + 1) * 128, :], o_.ap, reads=[o_])
        phase_end("out")

    seq = [phase_rope, phase_ln_in, lambda: phase_weights(0), lambda: phase_weights(1)]
    for l in range(2):
        seq += [lambda l=l: phase_inproj(l), lambda l=l: phase_hgrn_sgu(l), lambda l=l: phase_attn(l), lambda l=l: phase_ffn(l)]
    seq += [phase_out]
    with st:
        skip = os.environ.get('KSKIP', '')
        for fi, f in enumerate(seq):
            if str(fi) in skip.split(','):
                continue
            f()
            if stopped[0]:
                break
        P.emit(nc)
    return nc


_CACHE = {}


def make_in_maps(inputs, n_cores=8):
    cst = host_consts()
    maps = []
    for c in range(n_cores):
        b = c % 4
        m = {"x": np.ascontiguousarray(inputs["x"][b]),
             "p": np.ascontiguousarray(inputs["p"][:, b]),
             "pos": np.ascontiguousarray(inputs["positions"][b:b + 1]).astype(np.int32),
             "cst": cst}
        for k in ("ln_in_g", "ln_in_b", "w_in", "hgrn_lb_logits", "hgrn_norm_g", "sgu_ln_g", "sgu_ln_b", "sgu_w_s",
                  "sgu_b_s", "mla_q_norm_g", "mla_w_uq", "mla_kv_norm_g", "mla_w_ukv", "w_out", "ln1_g", "ln1_b",
                  "w_gate_up", "w_down", "ple_w_gate", "ple_w_proj", "ln2_g", "ln2_b"):
            m[k] = np.ascontiguousarray(inputs[k])
        maps.append(m)
    return maps


def kernel(**inputs):
    inputs = {k: np.asarray(v) for k, v in inputs.items()}
    if "nc" not in _CACHE:
        _CACHE["nc"] = build()
    nc = _CACHE["nc"]
    maps = make_in_maps(inputs, 4)
    res = run_bass_kernel_spmd(nc, maps, core_ids=list(range(4)))
    out = np.stack([np.asarray(res.results[b]["out"]) for b in range(4)], axis=0)
    return out.astype(np.float32)
```
